# Optimizing a Trainium2 kernel written in Bass

```python
import math
import jax, jax.numpy as jnp
from jax import lax
import numpy as np

D_MODEL = 1024
BATCH = 8
SEQ = 4096
DEPTH = 2

HEAD_DIM = 64
CONV_CH = D_MODEL // 4
MOBA_DIM = 3 * D_MODEL // 8
FOX_DIM = D_MODEL - CONV_CH - MOBA_DIM
MOBA_HEADS = MOBA_DIM // HEAD_DIM
FOX_HEADS = FOX_DIM // HEAD_DIM
MIX_DIM = CONV_CH + MOBA_DIM + FOX_DIM
CONV_WIDTH = 31
LN_EPS = 1e-5
RMS_EPS = 1e-6
MOBA_BLOCK = 256
MOBA_TOPK = 3
MOBA_Q_BLOCK = 64
FOX_Q_BLOCK = 128
NUM_BUCKETS = 32
MAX_DISTANCE = 128
D_FF = -(-8 * D_MODEL // (3 * 256)) * 256
NEG = -1e30

IN_WIDTHS = (CONV_CH, CONV_CH, MOBA_DIM, MOBA_DIM, MOBA_DIM, FOX_DIM, FOX_DIM, FOX_DIM, FOX_HEADS)
IN_DIM = 2 * CONV_CH + 3 * MOBA_DIM + 3 * FOX_DIM + FOX_HEADS
IN_SPLITS = tuple(int(s) for s in np.cumsum(IN_WIDTHS)[:-1])

kernel_name = "hybrid_conv_moba_fox_block"


def rms_norm(x, g):
    xf = x.astype(jnp.float32)
    y = xf * lax.rsqrt(jnp.mean(xf * xf, axis=-1, keepdims=True) + RMS_EPS)
    return (y * g.astype(jnp.float32)).astype(x.dtype)


def layer_norm(x, g, b):
    xf = x.astype(jnp.float32)
    mu = jnp.mean(xf, axis=-1, keepdims=True)
    var = jnp.mean(jnp.square(xf - mu), axis=-1, keepdims=True)
    y = (xf - mu) * lax.rsqrt(var + LN_EPS)
    return (y * g.astype(jnp.float32) + b.astype(jnp.float32)).astype(x.dtype)


def to_heads(t, n_heads):
    b, s, _ = t.shape
    return t.reshape(b, s, n_heads, HEAD_DIM).transpose(0, 2, 1, 3)


def from_heads(t):
    b, h, s, d = t.shape
    return t.transpose(0, 2, 1, 3).reshape(b, s, h * d)


def t5_bucket(dist):
    n = jnp.maximum(dist, 0)
    max_exact = NUM_BUCKETS // 2
    nf = jnp.maximum(n, 1).astype(jnp.float32)
    large = max_exact + (jnp.log(nf / max_exact) / math.log(MAX_DISTANCE / max_exact)
                         * (NUM_BUCKETS - max_exact)).astype(jnp.int32)
    large = jnp.minimum(large, NUM_BUCKETS - 1)
    return jnp.where(n < max_exact, n, large)


def conv_module(a_val, a_gate, w_dw, b_dw, ln_g, ln_b):
    h = a_val * jax.nn.sigmoid(a_gate)
    hp = jnp.pad(h, ((0, 0), (CONV_WIDTH - 1, 0), (0, 0)))
    y = lax.conv_general_dilated(hp, w_dw[:, None, :].astype(h.dtype), window_strides=(1,),
                                 padding='VALID', dimension_numbers=('NWC', 'WIO', 'NWC'),
                                 feature_group_count=h.shape[-1]) + b_dw
    y = layer_norm(y, ln_g, ln_b)
    return jax.nn.silu(y)


def moba_attention(q, k, v, bias_table):
    B, H, T, Dh = q.shape
    L = MOBA_BLOCK
    nb = -(-T // L)
    k_sel_n = min(MOBA_TOPK, nb)
    pad = nb * L - T
    kp = jnp.pad(k, ((0, 0), (0, 0), (0, pad), (0, 0)))
    vp = jnp.pad(v, ((0, 0), (0, 0), (0, pad), (0, 0)))
    kb = kp.reshape(B, H, nb, L, Dh)
    vb = vp.reshape(B, H, nb, L, Dh)
    kmean = jnp.mean(kb.astype(jnp.float32), axis=3).astype(q.dtype)
    scale = Dh ** -0.5
    table_h = bias_table.T.astype(jnp.float32)
    h_idx = jnp.arange(H)[None, :, None, None, None]
    offs = jnp.arange(L)
    n_chunks = T // MOBA_Q_BLOCK
    qc = q.reshape(B, H, n_chunks, MOBA_Q_BLOCK, Dh).transpose(2, 0, 1, 3, 4)
    gather_blocks = jax.vmap(jax.vmap(lambda blocks, ids: blocks[ids]))

    def chunk(args):
        qi, c = args
        t = c * MOBA_Q_BLOCK + jnp.arange(MOBA_Q_BLOCK)
        blk = (c * MOBA_Q_BLOCK) // L
        gate = jnp.einsum('bhqd,bhnd->bhqn', qi, kmean).astype(jnp.float32)
        gate = jnp.where(jnp.arange(nb) < blk, gate, -jnp.inf)
        _, idx = lax.top_k(gate, k_sel_n)
        slot_ok = jnp.arange(k_sel_n) < blk
        ksel = gather_blocks(kb, idx)
        vsel = gather_blocks(vb, idx)
        s_sel = jnp.einsum('bhqd,bhqkld->bhqkl', qi, ksel).astype(jnp.float32) * scale
        pos_sel = idx[..., None] * L + offs
        s_sel = s_sel + table_h[h_idx, t5_bucket(t[None, None, :, None, None] - pos_sel)]
        s_sel = jnp.where(slot_ok[:, None], s_sel, NEG)
        kown = lax.dynamic_index_in_dim(kb, blk, axis=2, keepdims=False)
        vown = lax.dynamic_index_in_dim(vb, blk, axis=2, keepdims=False)
        dist_own = t[:, None] - (blk * L + offs)[None, :]
        s_own = jnp.einsum('bhqd,bhld->bhql', qi, kown).astype(jnp.float32) * scale
        s_own = s_own + table_h[:, t5_bucket(dist_own)][None]
        s_own = jnp.where(dist_own >= 0, s_own, NEG)
        scores = jnp.concatenate([s_sel.reshape(B, H, MOBA_Q_BLOCK, k_sel_n * L), s_own], axis=-1)
        p = jax.nn.softmax(scores, axis=-1).astype(v.dtype)
        p_sel = p[..., :k_sel_n * L].reshape(B, H, MOBA_Q_BLOCK, k_sel_n, L)
        p_own = p[..., k_sel_n * L:]
        return (jnp.einsum('bhqkl,bhqkld->bhqd', p_sel, vsel)
                + jnp.einsum('bhql,bhld->bhqd', p_own, vown))

    out = lax.map(chunk, (qc, jnp.arange(n_chunks)))
    return out.transpose(1, 2, 0, 3, 4).reshape(B, H, T, Dh)


def fox_attention(q, k, v, log_f):
    B, H, T, Dh = q.shape
    scale = Dh ** -0.5
    csum = jnp.cumsum(log_f, axis=-1)
    n_chunks = T // FOX_Q_BLOCK
    qc = q.reshape(B, H, n_chunks, FOX_Q_BLOCK, Dh).transpose(2, 0, 1, 3, 4)
    cc = csum.reshape(B, H, n_chunks, FOX_Q_BLOCK).transpose(2, 0, 1, 3)
    s_pos = jnp.arange(T)

    def chunk(args):
        qi, ci, c = args
        t = c * FOX_Q_BLOCK + jnp.arange(FOX_Q_BLOCK)
        s = jnp.einsum('bhqd,bhsd->bhqs', qi, k).astype(jnp.float32) * scale
        s = s + ci[..., None] - csum[:, :, None, :]
        s = jnp.where(t[:, None] >= s_pos[None, :], s, NEG)
        p = jax.nn.softmax(s, axis=-1).astype(v.dtype)
        return jnp.einsum('bhqs,bhsd->bhqd', p, v)

    out = lax.map(chunk, (qc, cc, jnp.arange(n_chunks)))
    return out.transpose(1, 2, 0, 3, 4).reshape(B, H, T, Dh)


def setup_inputs(seed: int = 0) -> dict:
    key = jax.random.key(seed)
    ks = jax.random.split(key, 20)
    f32 = jnp.float32
    nrm = lambda k, shape, s: (jax.random.normal(k, shape, f32) * s)
    return {
        "x": nrm(ks[0], (BATCH, SEQ, D_MODEL), 1.0),
        "w_in": nrm(ks[1], (DEPTH, D_MODEL, IN_DIM), D_MODEL ** -0.5),
        "b_forget": 1.0 + nrm(ks[2], (DEPTH, FOX_HEADS), 0.1),
        "conv_w": nrm(ks[3], (DEPTH, CONV_WIDTH, CONV_CH), CONV_WIDTH ** -0.5),
        "conv_b": nrm(ks[4], (DEPTH, CONV_CH), 0.02),
        "conv_ln_g": 1.0 + nrm(ks[5], (DEPTH, CONV_CH), 0.02),
        "conv_ln_b": nrm(ks[6], (DEPTH, CONV_CH), 0.02),
        "moba_qn_g": 1.0 + nrm(ks[7], (DEPTH, HEAD_DIM), 0.02),
        "moba_kn_g": 1.0 + nrm(ks[8], (DEPTH, HEAD_DIM), 0.02),
        "fox_qn_g": 1.0 + nrm(ks[9], (DEPTH, HEAD_DIM), 0.02),
        "fox_kn_g": 1.0 + nrm(ks[10], (DEPTH, HEAD_DIM), 0.02),
        "rel_bias": nrm(ks[11], (NUM_BUCKETS, MOBA_HEADS), 0.5),
        "w_out": nrm(ks[12], (DEPTH, MIX_DIM, D_MODEL), MIX_DIM ** -0.5),
        "norm1_g": 1.0 + nrm(ks[13], (DEPTH, D_MODEL), 0.02),
        "norm2_g": 1.0 + nrm(ks[14], (DEPTH, D_MODEL), 0.02),
        "w_gate_up": nrm(ks[15], (DEPTH, D_MODEL, 2 * D_FF), D_MODEL ** -0.5),
        "w_down": nrm(ks[16], (DEPTH, D_FF, D_MODEL), D_FF ** -0.5),
    }


def reference(x, w_in, b_forget, conv_w, conv_b, conv_ln_g, conv_ln_b, moba_qn_g, moba_kn_g,
              fox_qn_g, fox_kn_g, rel_bias, w_out, norm1_g, norm2_g, w_gate_up, w_down):
    for l in range(DEPTH):
        h = rms_norm(x, norm1_g[l])
        proj = h @ w_in[l]
        a_val, a_gate, qb, kb, vb, qc, kc, vc, f_logit = jnp.split(proj, IN_SPLITS, axis=-1)
        y_a = conv_module(a_val, a_gate, conv_w[l], conv_b[l], conv_ln_g[l], conv_ln_b[l])
        qb_h = rms_norm(to_heads(qb, MOBA_HEADS), moba_qn_g[l])
        kb_h = rms_norm(to_heads(kb, MOBA_HEADS), moba_kn_g[l])
        y_b = from_heads(moba_attention(qb_h, kb_h, to_heads(vb, MOBA_HEADS), rel_bias))
        qc_h = rms_norm(to_heads(qc, FOX_HEADS), fox_qn_g[l])
        kc_h = rms_norm(to_heads(kc, FOX_HEADS), fox_kn_g[l])
        log_f = jax.nn.log_sigmoid(f_logit.astype(jnp.float32) + b_forget[l].astype(jnp.float32))
        y_c = from_heads(fox_attention(qc_h, kc_h, to_heads(vc, FOX_HEADS), log_f.transpose(0, 2, 1)))
        mixed = jnp.concatenate([y_a, y_b, y_c], axis=-1)
        x = x + mixed @ w_out[l]
        h2 = rms_norm(x, norm2_g[l])
        g, u = jnp.split(h2 @ w_gate_up[l], 2, axis=-1)
        x = x + (jax.nn.silu(g) * u) @ w_down[l]
    return x
```

```python
import math
from contextlib import ExitStack

import numpy as np
import ml_dtypes
import concourse.bass as bass
import concourse.mybir as mybir
from concourse.ap import AP
from concourse.bass_utils import run_bass_kernel_spmd

F32 = mybir.dt.float32
BF16 = mybir.dt.bfloat16
I32 = mybir.dt.int32
ALU = mybir.AluOpType
AF = mybir.ActivationFunctionType
AX = mybir.AxisListType

D = 1024
NCH = D // 128
IN_DIM = 2822
DFF = 2816
NFC = DFF // 128
CONVW = 31
NH = 6
NEGM = -30000.0
RMS_EPS = 1e-6
LN_EPS = 1e-5
C_AV, C_AG, C_QB, C_KB, C_VB, C_QC, C_KC, C_VC, C_F = 0, 256, 512, 896, 1280, 1664, 2048, 2432, 2816

ENGS = ("pe", "act", "dve", "pool", "sp")


class Buf:
    __slots__ = ("name", "w", "r")

    def __init__(self, name):
        self.name = name
        self.w = None
        self.r = []


class Sched:
    def __init__(self, nc):
        self.nc = nc
        self.eobj = {"pe": nc.tensor, "act": nc.scalar, "dve": nc.vector, "pool": nc.gpsimd, "sp": nc.sync}
        self.sem = {}
        self.cnt = {e: 0 for e in ENGS}
        for e in ("pe", "act", "dve", "pool"):
            self.sem[e] = nc.alloc_semaphore(name="sem_" + e)
        self.waited = {}
        self.dma_sems = {}
        self.nbuf = 0
        import os as _os
        self.maxops = int(_os.environ.get("MK_MAXOPS", "0")) or None
        self.nops = 0
        self.oplog = []

    def buf(self, name=None):
        self.nbuf += 1
        return Buf(name or ("b%d" % self.nbuf))

    def bufs(self, n, name="b"):
        return [self.buf("%s%d" % (name, i)) for i in range(n)]

    def _deps(self, reads, writes):
        deps = []
        for b in reads:
            if b.w is not None:
                deps.append(b.w)
        for b in writes:
            if b.w is not None:
                deps.append(b.w)
            deps.extend(b.r)
        return deps

    def _emit_waits(self, eng, deps, skip_self=False):
        need = {}
        for (kind, key, idx) in deps:
            if skip_self and kind == "eng" and key == eng:
                continue
            k = (kind, key)
            if need.get(k, -1) < idx:
                need[k] = idx
        for (kind, key), idx in need.items():
            if kind == "eng":
                val = idx + 1
                sem = self.sem[key]
            else:
                val = idx
                sem = self.dma_sems[key][0]
            wk = (eng, kind, key)
            if self.waited.get(wk, 0) >= val:
                continue
            self.waited[wk] = val
            self.eobj[eng].wait_ge(sem, val)

    def _mark(self, reads, writes, tag):
        for b in reads:
            b.r.append(tag)
            if len(b.r) > 64:
                b.r = self._compact(b.r)
        for b in writes:
            b.w = tag
            b.r = []

    @staticmethod
    def _compact(tags):
        best = {}
        for (kind, key, idx) in tags:
            k = (kind, key)
            if best.get(k, -1) < idx:
                best[k] = idx
        return [(k[0], k[1], v) for k, v in best.items()]

    def _skip(self, what):
        self.nops += 1
        if self.maxops is None:
            return False
        import sys as _sys
        fr = _sys._getframe(2)
        self.oplog.append((self.nops, what, fr.f_lineno))
        return self.nops > self.maxops

    def op(self, eng, fn, reads=(), writes=()):
        if self._skip(eng):
            return
        reads = [b for b in reads if b is not None]
        writes = [b for b in writes if b is not None]
        self._emit_waits(eng, self._deps(reads, writes), skip_self=(eng == "pe"))
        idx = self.cnt[eng]
        self.cnt[eng] += 1
        sem = self.sem[eng]
        fn(self.eobj[eng]).then_inc(sem, 1)
        self._mark(reads, writes, ("eng", eng, idx))

    def dma(self, queue, slot, out, in_, reads=(), writes=(), **kw):
        if self._skip("dma:" + queue):
            return
        reads = [b for b in reads if b is not None]
        writes = [b for b in writes if b is not None]
        if slot not in self.dma_sems:
            self.dma_sems[slot] = [self.nc.alloc_semaphore(name="dsem_" + slot), 0]
        ent = self.dma_sems[slot]
        self._emit_waits(queue, self._deps(reads, writes))
        ent[1] += 16
        val = ent[1]
        sem = ent[0]
        self.eobj[queue].dma_start(out=out, in_=in_, **kw).then_inc(sem, 16)
        self._mark(reads, writes, ("dma", slot, val))

    def barrier(self):
        deps = [("eng", e, self.cnt[e] - 1) for e in ("pe", "act", "dve", "pool") if self.cnt[e] > 0]
        deps += [("dma", s, ent[1]) for s, ent in self.dma_sems.items() if ent[1] > 0]
        for e in ENGS:
            self._emit_waits(e, deps, skip_self=False)

    def emit(self):
        if self.maxops is not None:
            for o in self.oplog[-6:] if self.maxops >= self.nops else self.oplog[max(0, self.maxops - 3):self.maxops + 2]:
                print("OPLOG", o)
            print("OPLOG total", self.nops)
        return


def _t5_bucket(n):
    n = np.maximum(n, 0)
    nf = np.maximum(n, 1).astype(np.float32)
    large = 16 + (np.log(nf / np.float32(16)) / np.float32(math.log(128 / 16)) * np.float32(16)).astype(np.int32)
    large = np.minimum(large, 31)
    return np.where(n < 16, n, large)


def make_consts(T):
    bf = ml_dtypes.bfloat16
    c = {}
    c["identb"] = np.eye(128, dtype=np.float32).astype(bf)
    c["identf"] = np.eye(128, dtype=np.float32)
    c["antij"] = np.eye(128, dtype=np.float32)[::-1].copy().astype(bf)
    c["onesf"] = np.ones((128, 128), np.float32)
    sel = np.zeros((128, 6, 12), np.float32)
    exp = np.zeros((12, 6, 128), np.float32)
    for i in range(6):
        sel[:64, i, 2 * i] = 1
        sel[64:, i, 2 * i + 1] = 1
        exp[2 * i, i, :64] = 1
        exp[2 * i + 1, i, 64:] = 1
    c["qksel"] = sel.astype(bf)
    c["qkexp"] = exp.astype(bf)
    E = np.zeros((33, 640), np.float32)
    idx = np.arange(640)
    d = idx - 255
    bk = _t5_bucket(d)
    for i in range(640):
        if d[i] >= 0:
            E[bk[i], i] += 1.0
            E[31, i] -= 1.0
        else:
            E[32, i] = NEGM
    c["erow"] = E
    pen = np.zeros((128, 16, 16), np.float32)
    for blk in range(16):
        pen[:, blk, blk:] = -1e30
    c["pen"] = pen
    oh = np.zeros((16, T), np.float32)
    for n in range(T // 256):
        oh[n, n * 256:(n + 1) * 256] = 1
    c["blkoh"] = oh.astype(bf)
    c["ones3"] = np.ones((3, T), np.float32).astype(bf)
    p = np.arange(128)[:, None]
    x = np.arange(896)[None, :]
    c["gm"] = np.where(x + p - 511 >= 0, 0.0, NEGM).astype(np.float32).astype(bf)
    c["tri"] = (np.arange(128)[:, None] <= np.arange(128)[None, :]).astype(np.float32)
    return c


CONST_SPECS = [("identb", (128, 128), BF16), ("identf", (128, 128), F32), ("antij", (128, 128), BF16),
               ("onesf", (128, 128), F32), ("qksel", (128, 6, 12), BF16), ("qkexp", (12, 6, 128), BF16),
               ("erow", (33, 640), F32), ("pen", (128, 16, 16), F32), ("blkoh", None, BF16),
               ("ones3", None, BF16), ("gm", (128, 896), BF16), ("tri", (128, 128), F32)]

PARAM_SPECS = [("w_in", (2, D, IN_DIM)), ("b_forget", (2, 6)), ("conv_w", (2, CONVW, 256)), ("conv_b", (2, 256)),
               ("conv_ln_g", (2, 256)), ("conv_ln_b", (2, 256)), ("moba_qn_g", (2, 64)), ("moba_kn_g", (2, 64)),
               ("fox_qn_g", (2, 64)), ("fox_kn_g", (2, 64)), ("rel_bias", (32, 6)), ("w_out", (2, D, D)),
               ("norm1_g", (2, D)), ("norm2_g", (2, D)), ("w_gate_up", (2, D, 2 * DFF)), ("w_down", (2, DFF, D))]


def build_program(T=4096, depth=2, phases="ABCD", dbg=()):
    assert T % 512 == 0
    NT = T // 512
    NJ = T // 128
    NB = T // 256
    nc = bass.Bass("TRN2", target_bir_lowering=False)
    S = Sched(nc)

    def din(name, shape, dt=F32):
        return nc.dram_tensor(name, list(shape), dt, kind="ExternalInput").ap()

    x_in = din("x", (T, D))
    P = {n: din(n, s) for n, s in PARAM_SPECS}
    C = {}
    for n, s, dt in CONST_SPECS:
        if n == "blkoh":
            s = (16, T)
        if n == "ones3":
            s = (3, T)
        C[n] = din("c_" + n, s, dt)
    out = nc.dram_tensor("out", [T, D], F32, kind="ExternalOutput").ap()
    dbg_out = {}
    for name, shape, dt in dbg:
        dbg_out[name] = nc.dram_tensor("dbg_" + name, list(shape), dt, kind="ExternalOutput").ap()

    def dscr(name, shape, dt):
        return nc.dram_tensor(name, list(shape), dt).ap()

    xa = dscr("xa", (T, D), F32)
    xb = dscr("xb", (T, D), F32)
    qm = dscr("qm", (NH, 64, T), BF16)
    nm = dscr("nm", (NH, 16, T), BF16)
    km = dscr("km", (NH, 64, T), BF16)
    qf = dscr("qf", (NH, 64, T), BF16)
    kf = dscr("kf", (NH, 64, T), BF16)
    cf = dscr("cf", (NH * 3 * NJ, 128), BF16)
    vs = dscr("vs", (T, 768), BF16)
    rrow = dscr("rrow", (NH, 640), F32)

    with ExitStack() as top:
        _nm = [0]

        def sb(es, name, shape, dt):
            _nm[0] += 1
            return es.enter_context(nc.sbuf_tensor("%s_%d" % (name, _nm[0]), list(shape), dt))

        psb = [top.enter_context(nc.psum_tensor("ps%d" % i, [128, 512], F32)) for i in range(8)]
        psbuf = S.bufs(8, "ps")
        pools = {"all": list(range(8))}
        prr = {}

        def set_pools(**kw):
            pools.clear()
            pools.update(kw)
            prr.clear()

        def psum(pool="all"):
            lst = pools[pool]
            i = prr.get(pool, 0)
            prr[pool] = (i + 1) % len(lst)
            k = lst[i]
            return psb[k], psbuf[k]

        identb = sb(top, "identb", (128, 128), BF16)
        identf = sb(top, "identf", (128, 128), F32)
        antij = sb(top, "antij", (128, 128), BF16)
        onesf = sb(top, "onesf", (128, 128), F32)
        qksel = sb(top, "qksel", (128, 6, 12), BF16)
        qkexp = sb(top, "qkexp", (12, 6, 128), BF16)
        pen = sb(top, "pen", (128, 16, 16), F32)
        gm = sb(top, "gm", (128, 896), BF16)
        tri = sb(top, "tri", (128, 128), F32)
        prmT = sb(top, "prmT", (128, 176), F32)
        bfb = sb(top, "bfb", (128, 2, 6), F32)
        gt = sb(top, "gt", (128, NH, 512), BF16)
        ones_row = sb(top, "ones_row", (128, 32), F32)
        b_const = S.buf("const")
        b_gt = S.buf("gt")

        ci = 0
        for nm_, t_ in (("identb", identb), ("identf", identf), ("antij", antij), ("onesf", onesf),
                        ("qksel", qksel), ("qkexp", qkexp), ("pen", pen), ("gm", gm), ("tri", tri)):
            S.dma("sp", "c%d" % ci, t_[:], C[nm_], writes=[b_const])
            ci += 1
        S.dma("sp", "c%d" % ci, bfb[:], P["b_forget"].rearrange("l h -> (l h)").partition_broadcast(128)
              .rearrange("p (l h) -> p l h", l=2), writes=[b_const])
        ci += 1
        S.op("dve", lambda e: e.memset(ones_row[:], 1.0), writes=[b_const])

        with ExitStack() as es0:
            prmA = sb(es0, "prmA", (128, 128), F32)
            prmB = sb(es0, "prmB", (48, 128), F32)
            erow = sb(es0, "erow", (33, 640), F32)
            tab = sb(es0, "tab", (33, NH), F32)
            rsb = sb(es0, "rsb", (NH, 640), F32)
            gst = sb(es0, "gst", (128, 512), F32)
            b_prm, b_tab, b_rsb, b_gst, b_rrow = S.bufs(5, "st")
            S.dma("sp", "p0", prmA[0:16, :], P["norm1_g"].rearrange("l (c p) -> (l c) p", p=128), writes=[b_prm])
            S.dma("sp", "p1", prmA[16:32, :], P["norm2_g"].rearrange("l (c p) -> (l c) p", p=128), writes=[b_prm])
            S.dma("sp", "p2", prmA[32:36, :], P["conv_b"].rearrange("l (c p) -> (l c) p", p=128), writes=[b_prm])
            S.dma("sp", "p3", prmA[36:40, :], P["conv_ln_g"].rearrange("l (c p) -> (l c) p", p=128), writes=[b_prm])
            S.dma("sp", "p4", prmA[40:44, :], P["conv_ln_b"].rearrange("l (c p) -> (l c) p", p=128), writes=[b_prm])
            k = 5
            for l in range(2):
                for gi, gname in enumerate(("moba_qn_g", "moba_kn_g", "fox_qn_g", "fox_kn_g")):
                    r = 44 + l * 4 + gi
                    for hf in range(2):
                        S.dma("sp", "p%d" % k, prmA[r:r + 1, hf * 64:(hf + 1) * 64], P[gname][l:l + 1, :], writes=[b_prm])
                        k += 1
            cwv = P["conv_w"].rearrange("l k (c p) -> (l k c) p", p=128)
            S.dma("sp", "p%d" % k, prmA[52:128, :], cwv[0:76, :], writes=[b_prm])
            k += 1
            S.dma("sp", "p%d" % k, prmB[:, :], cwv[76:124, :], writes=[b_prm])
            k += 1
            S.dma("sp", "p%d" % k, erow[:], C["erow"], writes=[b_prm])
            k += 1
            S.op("dve", lambda e: e.memset(tab[32:33, :], 1.0), writes=[b_tab])
            S.dma("sp", "p%d" % k, tab[0:32, :], P["rel_bias"], reads=[b_tab], writes=[b_tab])
            k += 1
            pt, bpt = psum()
            S.op("pe", lambda e: e.transpose(pt[:, 0:128], prmA[:, :], identf[:, :]),
                 reads=[b_prm, b_const], writes=[bpt])
            S.op("act", lambda e: e.activation(prmT[:, 0:128], pt[:, 0:128], AF.Copy), reads=[bpt], writes=[b_const])
            pt2, bpt2 = psum()
            S.op("pe", lambda e: e.transpose(pt2[:, 0:48], prmB[:, :], identf[0:48, 0:48]),
                 reads=[b_prm, b_const], writes=[bpt2])
            S.op("act", lambda e: e.activation(prmT[:, 128:176], pt2[:, 0:48], AF.Copy), reads=[bpt2], writes=[b_const])
            pr0, bpr0 = psum()
            pr1, bpr1 = psum()
            S.op("pe", lambda e: e.matmul(pr0[0:NH, 0:320], tab[:, :], erow[:, 0:320], start=True, stop=True),
                 reads=[b_tab, b_prm], writes=[bpr0])
            S.op("pe", lambda e: e.matmul(pr1[0:NH, 0:320], tab[:, :], erow[:, 320:640], start=True, stop=True),
                 reads=[b_tab, b_prm], writes=[bpr1])
            S.op("act", lambda e: e.activation(rsb[:, 0:320], pr0[0:NH, 0:320], AF.Copy), reads=[bpr0], writes=[b_rsb])
            S.op("act", lambda e: e.activation(rsb[:, 320:640], pr1[0:NH, 0:320], AF.Copy), reads=[bpr1], writes=[b_rsb])
            S.dma("sp", "p%d" % k, rrow, rsb[:], reads=[b_rsb], writes=[b_rrow])
            k += 1
            for h in range(NH):
                skew = AP(rrow.tensor, h * 640, [[1, 128], [1, 512]])
                S.dma("sp", "gsk", gst[:], skew, reads=[b_rrow], writes=[b_gst])
                S.op("act", lambda e, h=h: e.activation(gt[:, h, :], gst[:], AF.Copy), reads=[b_gst], writes=[b_gt])
            S.barrier()

        if "gt" in dbg_out:
            S.dma("sp", "dbg", dbg_out["gt"], gt[:].rearrange("p h x -> p (h x)"), reads=[b_gt])
            S.dma("sp", "dbg", dbg_out["prmT"], prmT[:], reads=[b_const])

        def rsqrt_newton(v, y, ti, t1, bv, by, bti, bt1, iters=2):
            S.op("dve", lambda e: e.tensor_scalar(ti, v.bitcast(I32), 9, None, ALU.arith_shift_right),
                 reads=[bv], writes=[bti])
            S.op("dve", lambda e: e.tensor_scalar(ti, ti, -1.0, float(0x5f3759df >> 8), ALU.mult, ALU.add),
                 reads=[bti], writes=[bti])
            S.op("dve", lambda e: e.tensor_scalar(y.bitcast(I32), ti, 8, None, ALU.logical_shift_left),
                 reads=[bti], writes=[by])
            for _ in range(iters):
                S.op("dve", lambda e: e.tensor_tensor(t1, y, y, ALU.mult), reads=[by], writes=[bt1])
                S.op("dve", lambda e: e.tensor_tensor(t1, t1, v, ALU.mult), reads=[bt1, bv], writes=[bt1])
                S.op("dve", lambda e: e.tensor_scalar(t1, t1, -0.5, 1.5, ALU.mult, ALU.add), reads=[bt1], writes=[bt1])
                S.op("dve", lambda e: e.tensor_tensor(y, y, t1, ALU.mult), reads=[by, bt1], writes=[by])

        def norm_part1(xs_aps, b_xs_list, nsub, tmp):
            ssq, rstd, rti, rt1, junk, hb = tmp["ssq"], tmp["rstd"], tmp["rti"], tmp["rt1"], tmp["junk"], tmp["hb"]
            bs = tmp["b"]
            for j in range(nsub):
                S.op("act", lambda e, j=j: e.activation(junk[:], xs_aps[j], AF.Square, accum_out=ssq[:, j:j + 1]),
                     reads=[b_xs_list[j]], writes=[bs["junk"], bs["ssq"]])
            S.op("dve", lambda e: e.tensor_scalar(ssq[:, 0:nsub], ssq[:, 0:nsub], 1.0 / D, RMS_EPS, ALU.mult, ALU.add),
                 reads=[bs["ssq"]], writes=[bs["ssq"]])
            rsqrt_newton(ssq[:, 0:nsub], rstd[:, 0:nsub], rti[:, 0:nsub], rt1[:, 0:nsub],
                         bs["ssq"], bs["rstd"], bs["rti"], bs["rt1"], iters=3)
            for j in range(nsub):
                S.op("dve", lambda e, j=j: e.tensor_scalar(hb[j][:], xs_aps[j], rstd[:, j:j + 1], None, ALU.mult),
                     reads=[b_xs_list[j], bs["rstd"]], writes=[bs["hb"][j]])

        def norm_part2(nsub, hT_t, b_hT, tmp):
            hb = tmp["hb"]
            bs = tmp["b"]
            for j in range(nsub):
                pt_, bpt_ = psum("tp")
                ptb = pt_[:].bitcast(BF16)

                def tr(e, j=j, ptb=ptb):
                    ins = None
                    for c in range(NCH):
                        ins = e.transpose(ptb[:, c * 128:(c + 1) * 128], hb[j][:, c * 128:(c + 1) * 128], identb[:, :])
                    return ins
                S.op("pe", tr, reads=[bs["hb"][j], b_const], writes=[bpt_])
                S.op("act", lambda e, j=j, ptb=ptb: e.activation(
                    hT_t[:, :, j * 128:(j + 1) * 128], ptb.rearrange("p (c t) -> p c t", c=NCH), AF.Copy),
                    reads=[bpt_], writes=[b_hT])

        def make_norm_tmp(es, nsub):
            return dict(ssq=sb(es, "ssq", (128, 4), F32), rstd=sb(es, "rstd", (128, 4), F32),
                        rti=sb(es, "rti", (128, 4), I32), rt1=sb(es, "rt1", (128, 4), F32),
                        junk=sb(es, "junk", (128, D), BF16),
                        hb=[sb(es, "hb%d" % i, (128, D), BF16) for i in range(nsub)],
                        b=dict(ssq=S.buf(), rstd=S.buf(), rti=S.buf(), rt1=S.buf(), junk=S.buf(), hb=S.bufs(nsub)))

        mixA = sb(top, "mixA", (128, 2, T), BF16)
        b_mixA = S.buf("mixA")
        negc = sb(top, "negc", (128, NJ, NH), F32)
        b_negc = S.buf("negc")

        for l in range(depth):
            x_src = x_in if l == 0 else xb
            x_dst = xb if l == 0 else out
            if l == depth - 1:
                x_dst = out
            if "A" in phases:
                with ExitStack() as es:
                    set_pools(tp=[0, 1], pj=[2, 3, 4, 7], ss=[5, 6])
                    wb = sb(es, "wb", (128, NCH, IN_DIM), BF16)
                    pc = 0 + l * NCH
                    wgrp = [(C_QB, C_VB), (C_AV, C_QB), (C_QC, C_VC), (C_VB, C_QC), (C_VC, IN_DIM)]
                    b_wg = S.bufs(len(wgrp), "wbg")
                    w_in_v = P["w_in"][l].rearrange("(c p) f -> p c f", p=128)
                    for gi_, (c0_, c1_) in enumerate(wgrp):
                        S.dma("pool", "wb%d" % gi_, wb[:, :, c0_:c1_], w_in_v[:, :, c0_:c1_],
                              writes=[b_wg[gi_]], max_dma_last_dim=4096)

                    def wbuf(col0):
                        for gi_, (c0_, c1_) in enumerate(wgrp):
                            if c0_ <= col0 < c1_:
                                return b_wg[gi_]
                        raise ValueError(col0)
                    xq = [sb(es, "xq%d" % i, (128, D), F32) for i in range(4)]
                    b_xq = S.bufs(4, "xq")
                    hT = [sb(es, "hT%d" % i, (128, NCH, 512), BF16) for i in range(2)]
                    b_hT = S.bufs(2, "hT")
                    tmp = make_norm_tmp(es, 4)
                    hglu = sb(es, "hglu", (128, 2, 2, 544), BF16)
                    b_hglu = S.bufs(2, "hglu")
                    diag = sb(es, "diag", (128, 2, CONVW, 128), BF16)
                    b_diag = S.buf("diag")
                    cwh = sb(es, "cwh", (128, 62), F32)
                    gq8 = sb(es, "gq8", (128, 4), F32)
                    b_sm = S.buf("small")
                    kmean = sb(es, "kmean", (128, 3, 16), F32)
                    kmeanb = sb(es, "kmeanb", (128, 3, 16), BF16)
                    b_kmean = S.buf("kmean")
                    fl = sb(es, "fl", (128, NJ, NH), F32)
                    b_fl = S.buf("fl")
                    qraw = [[[sb(es, "qraw%d_%d_%d" % (p_, g, i), (128, 512), BF16) for i in range(6)] for g in range(2)] for p_ in range(2)]
                    b_qraw = [[S.bufs(6, "qraw%d_%d_" % (p_, g)) for g in range(2)] for p_ in range(2)]
                    qsv = [[sb(es, "qsv%d_%d" % (p_, g), (128, 48), F32) for g in range(2)] for p_ in range(2)]
                    b_qsv = [S.bufs(2, "qsv%d_" % p_) for p_ in range(2)]
                    sqb = [sb(es, "sq%d" % i, (128, 512), BF16) for i in range(3)]
                    b_sq = S.bufs(3, "sq")
                    qn = [sb(es, "qn%d" % i, (128, 512), BF16) for i in range(6)]
                    b_qn = S.bufs(6, "qn")
                    qs = [[{k_: sb(es, "qs%d_%d_%s" % (p_, g, k_), (128, 48), dt_) for k_, dt_ in
                            (("y", F32), ("ti", I32), ("t1", F32), ("hi", BF16), ("lo", BF16))} for g in range(2)] for p_ in range(2)]
                    b_qs = [[{k_: S.buf("qs%d_%d_%s" % (p_, g, k_)) for k_ in qs[p_][g]} for g in range(2)] for p_ in range(2)]
                    rsT = [sb(es, "rsT%d" % g, (12, 1024), BF16) for g in range(2)]
                    b_rsT = S.bufs(2, "rsT")
                    th = [sb(es, "th%d" % i, (128, 512), F32) for i in range(2)]
                    b_th = S.bufs(2, "th")
                    ycv = [sb(es, "ycv%d" % i, (128, 512), F32) for i in range(2)]
                    b_ycv = S.bufs(2, "ycv")
                    ysq = th
                    b_ysq = b_th
                    lns = {k_: sb(es, "lns_" + k_, (128, 8), dt_) for k_, dt_ in
                           (("s", F32), ("v", F32), ("y", F32), ("ti", I32), ("t1", F32), ("ab", F32), ("abf", F32), ("hi", BF16), ("lo", BF16))}
                    b_lns = {k_: S.buf("lns_" + k_) for k_ in lns}
                    abT = sb(es, "abT", (2, 1024), BF16)
                    b_abT = S.buf("abT")
                    selab = sb(es, "selab", (2, 2, 128), BF16)
                    onescol = sb(es, "onescol", (128, 1), F32)
                    vst = [sb(es, "vst%d" % i, (128, 768), BF16) for i in range(2)]
                    b_vst = S.bufs(2, "vst")
                    gsb = sb(es, "gsb", (128, 4, NH, 16), F32)
                    m8 = sb(es, "m8", (128, 4, NH, 8), F32)
                    nmk = sb(es, "nmk", (128, 4, NH, 16), F32)
                    nmb = sb(es, "nmb", (128, 4, NH * 16), BF16)
                    nmT = sb(es, "nmT", (96, 512), BF16)
                    b_gsb, b_m8, b_nmk, b_nmb, b_nmT = S.bufs(5, "gate")

                    S.op("dve", lambda e: e.tensor_scalar(cwh[:], prmT[:, 52 + l * 62:52 + (l + 1) * 62], 0.5, None, ALU.mult),
                         reads=[b_const], writes=[b_sm])
                    gb = 44 + l * 4
                    S.op("dve", lambda e: e.tensor_scalar(gq8[:, 0:1], prmT[:, gb:gb + 1], 0.125, None, ALU.mult), reads=[b_const], writes=[b_sm])
                    S.op("dve", lambda e: e.tensor_copy(gq8[:, 1:2], prmT[:, gb + 1:gb + 2]), reads=[b_const], writes=[b_sm])
                    S.op("dve", lambda e: e.tensor_scalar(gq8[:, 2:3], prmT[:, gb + 2:gb + 3], 0.125, None, ALU.mult), reads=[b_const], writes=[b_sm])
                    S.op("dve", lambda e: e.tensor_copy(gq8[:, 3:4], prmT[:, gb + 3:gb + 4]), reads=[b_const], writes=[b_sm])
                    S.op("dve", lambda e: e.memset(onescol[:], 1.0), writes=[b_sm])
                    S.op("dve", lambda e: e.memset(selab[:], 0.0), writes=[b_sm])
                    S.op("dve", lambda e: e.memset(selab[0:1, 0, :], 1.0), writes=[b_sm])
                    S.dma("sp", "selab", selab[1:2, 1, :], selab[0:1, 0, :], reads=[b_sm], writes=[b_sm])
                    for k_ in range(CONVW):
                        for ch in range(2):
                            S.op("act", lambda e, k_=k_, ch=ch: e.activation(
                                diag[:, ch, k_, :], identf[:, :], AF.Copy, scale=cwh[:, k_ * 2 + ch:k_ * 2 + ch + 1]),
                                reads=[b_const, b_sm], writes=[b_diag])
                    S.op("pool", lambda e: e.memset(hglu[:, 0, :, 0:32], 0.0), writes=[b_hglu[0]])
                    S.op("pool", lambda e: e.memset(kmean[:], 0.0), writes=[b_kmean])
                    S.op("pool", lambda e: e.memset(kmeanb[:], 0.0), writes=[b_kmean])

                    def load_x(tt):
                        for j in range(4):
                            r0 = tt * 512 + j * 128
                            S.dma("sp", "xq%d" % j, xq[j][:], x_src[r0:r0 + 128, :], writes=[b_xq[j]])

                    def s1_part1(tt):
                        norm_part1([xq[j][:] for j in range(4)], b_xq, 4, tmp)
                        if tt + 1 < NT:
                            load_x(tt + 1)

                    def s1_part2(tt):
                        norm_part2(4, hT[tt % 2], b_hT[tt % 2], tmp)

                    def proj_fm(tt, col0):
                        hTc, bhTc = hT[tt % 2], b_hT[tt % 2]
                        ps_, bps_ = psum("pj")

                        def f(e):
                            ins = None
                            for c in range(NCH):
                                ins = e.matmul(ps_[:, :], wb[:, c, col0:col0 + 128], hTc[:, c, :],
                                               start=(c == 0), stop=(c == NCH - 1))
                            return ins
                        S.op("pe", f, reads=[bhTc, wbuf(col0)], writes=[bps_])
                        return ps_, bps_

                    sqi = [0]
                    pssT = [None, None]

                    def qk_proj(tt, grp):
                        qcol = C_QB if grp == 0 else C_QC
                        kcol = C_KB if grp == 0 else C_KC
                        pss, bpss = psum("ss")
                        pend = []

                        def emit_fss(i, sq_, bsq_):
                            def fss(e):
                                ins = None
                                for j in range(4):
                                    ins = e.matmul(pss[:, j * 12:(j + 1) * 12], sq_[:, j * 128:(j + 1) * 128], qksel[:, i, :],
                                                   start=(i == 0 and j == 0), stop=(i == 5 and j == 3), skip_group_check=True)
                                return ins
                            S.op("pe", fss, reads=[bsq_, b_const], writes=[bpss])
                        for i in range(6):
                            col0 = (qcol + i * 128) if i < 3 else (kcol + (i - 3) * 128)
                            pq_, bpq_ = proj_fm(tt, col0)
                            sq_ = sqb[sqi[0] % 3]
                            bsq_ = b_sq[sqi[0] % 3]
                            sqi[0] += 1
                            S.op("act", lambda e: e.activation(sq_[:], pq_[:, :], AF.Square), reads=[bpq_], writes=[bsq_])
                            S.op("act", lambda e: e.activation(qraw[tt % 2][grp][i][:], pq_[:, :], AF.Copy), reads=[bpq_],
                                 writes=[b_qraw[tt % 2][grp][i]])
                            if pend:
                                emit_fss(*pend.pop(0))
                            pend.append((i, sq_, bsq_))
                        while pend:
                            emit_fss(*pend.pop(0))
                        S.op("dve", lambda e: e.tensor_scalar(qsv[tt % 2][grp][:], pss[:, 0:48], 1.0 / 64, RMS_EPS, ALU.mult, ALU.add),
                             reads=[bpss], writes=[b_qsv[tt % 2][grp]])
                        q_, bq_ = qs[tt % 2][grp], b_qs[tt % 2][grp]
                        v_, bv_ = qsv[tt % 2][grp], b_qsv[tt % 2][grp]
                        rsqrt_newton(v_[:], q_["y"][:], q_["ti"][:], q_["t1"][:], bv_, bq_["y"], bq_["ti"], bq_["t1"], iters=2)
                        S.op("dve", lambda e: e.tensor_copy(q_["hi"][:], q_["y"][:]), reads=[bq_["y"]], writes=[bq_["hi"]])
                        S.op("dve", lambda e: e.tensor_copy(q_["t1"][:], q_["hi"][:]), reads=[bq_["hi"]], writes=[bq_["t1"]])
                        S.op("dve", lambda e: e.tensor_tensor(q_["lo"][:], q_["y"][:], q_["t1"][:], ALU.subtract),
                             reads=[bq_["y"], bq_["t1"]], writes=[bq_["lo"]])

                    def qk_chain1(tt, grp):
                        q_, bq_ = qs[tt % 2][grp], b_qs[tt % 2][grp]
                        ptr, bptr = psum("tp")
                        ptrb = ptr[:].bitcast(BF16)

                        def ftr(e):
                            ins = None
                            for w_, key in enumerate(("hi", "lo")):
                                for j in range(4):
                                    ins = e.transpose(ptrb[0:12, w_ * 512 + j * 128:w_ * 512 + (j + 1) * 128],
                                                      q_[key][:, j * 12:(j + 1) * 12], identb[:, :])
                            return ins
                        S.op("pe", ftr, reads=[bq_["hi"], bq_["lo"], b_const], writes=[bptr])
                        S.op("act", lambda e: e.activation(rsT[grp][:, :], ptrb[0:12, :], AF.Copy), reads=[bptr], writes=[b_rsT[grp]])

                    def qk_chain2(tt, grp):
                        t0 = tt * 512
                        for i in range(6):
                            pbc, bpbc = psum("pj")

                            def fbc(e):
                                e.matmul(pbc[:, :], qkexp[:, i, :], rsT[grp][:, 0:512], start=True, stop=False)
                                return e.matmul(pbc[:, :], qkexp[:, i, :], rsT[grp][:, 512:1024], start=False, stop=True)
                            S.op("pe", fbc, reads=[b_rsT[grp], b_const], writes=[bpbc])
                            gcol = (0 if i < 3 else 1) + 2 * grp
                            S.op("dve", lambda e: e.scalar_tensor_tensor(
                                qn[i][:], pbc[:, :], gq8[:, gcol:gcol + 1], qraw[tt % 2][grp][i][:], ALU.mult, ALU.mult),
                                reads=[bpbc, b_sm, b_qraw[tt % 2][grp][i]], writes=[b_qn[i]])
                            if grp == 0 and i >= 3:
                                S.op("dve", lambda e: e.tensor_reduce(
                                    kmean[:, i - 3, 2 * tt:2 * tt + 2], qn[i][:].rearrange("p (b s) -> p b s", b=2),
                                    AX.X, ALU.add), reads=[b_qn[i]], writes=[b_kmean])
                            if grp == 0 and i == 2:
                                gate_q = True
                            dst = (qm, km, qf, kf)[(0 if i < 3 else 1) + 2 * grp]
                            hp = (i % 3) * 2
                            S.dma("sp", "qn%d" % i, dst[hp:hp + 2, :, t0:t0 + 512].rearrange("h d t -> (h d) t"),
                                  qn[i][:], reads=[b_qn[i]])

                    def gate(tt):
                        t0 = tt * 512
                        S.op("dve", lambda e: e.tensor_scalar(kmeanb[:, :, 2 * tt:2 * tt + 2], kmean[:, :, 2 * tt:2 * tt + 2],
                                                               1.0 / 256, None, ALU.mult),
                             reads=[b_kmean], writes=[b_kmean])
                        if tt > 0:
                            pg2 = [psum("pj"), psum("pj")]

                            def fg(e):
                                ins = None
                                for hf in range(2):
                                    for j in range(4):
                                        for i_ in range(3):
                                            o_ = (j * 3 + i_) * 16
                                            ins = e.matmul(pg2[hf][0][:, o_:o_ + 16],
                                                           qn[i_][hf * 64:(hf + 1) * 64, j * 128:(j + 1) * 128],
                                                           kmeanb[hf * 64:(hf + 1) * 64, i_, :], start=True, stop=True)
                                return ins
                            S.op("pe", fg, reads=[b_qn[0], b_qn[1], b_qn[2], b_kmean], writes=[pg2[0][1], pg2[1][1]])
                            for j in range(4):
                                blk = 2 * tt + j // 2
                                for hf in range(2):
                                    S.op("dve", lambda e, j=j, hf=hf, blk=blk: e.tensor_tensor(
                                        gsb[:, j, hf * 3:(hf + 1) * 3, :],
                                        pg2[hf][0][:, j * 48:(j + 1) * 48].rearrange("p (h n) -> p h n", h=3),
                                        pen[:, blk, :].unsqueeze(1).to_broadcast([128, 3, 16]), ALU.add),
                                        reads=[pg2[hf][1], b_const], writes=[b_gsb])
                            for j in range(4):
                                for h in range(NH):
                                    S.op("dve", lambda e, j=j, h=h: e.max(m8[:, j, h, :], gsb[:, j, h, :]),
                                         reads=[b_gsb], writes=[b_m8])
                            S.op("dve", lambda e: e.tensor_tensor(
                                nmk[:].rearrange("p j h n -> p (j h) n"), gsb[:].rearrange("p j h n -> p (j h) n"),
                                m8[:].rearrange("p j h n -> p (j h) n")[:, :, 2:3].to_broadcast([128, 4 * NH, 16]),
                                ALU.is_lt), reads=[b_gsb, b_m8], writes=[b_nmk])
                            nmb5 = nmb[:].rearrange("p j (i f n) -> p j i f n", i=3, f=2)
                            for hf in range(2):
                                S.op("dve", lambda e, hf=hf: e.tensor_scalar(
                                    nmb5[:, :, :, hf, :], nmk[:, :, hf * 3:(hf + 1) * 3, :],
                                    NEGM, None, ALU.mult), reads=[b_nmk], writes=[b_nmb])
                        else:
                            S.op("dve", lambda e: e.memset(nmb[:], 0.0), writes=[b_nmb])

                    def gate2(tt):
                        t0 = tt * 512
                        ptn, bptn = psum("tp")
                        ptnb = ptn[:].bitcast(BF16)

                        def ftn(e):
                            ins = None
                            for j in range(4):
                                ins = e.transpose(ptnb[0:96, j * 128:(j + 1) * 128], nmb[:, j, :], identb[:, :])
                            return ins
                        S.op("pe", ftn, reads=[b_nmb, b_const], writes=[bptn])
                        S.op("act", lambda e: e.activation(nmT[:, :], ptnb[0:96, 0:512], AF.Copy), reads=[bptn], writes=[b_nmT])
                        S.dma("sp", "nmT", nm[:, :, t0:t0 + 512].rearrange("h n t -> (h n) t"), nmT[:, :], reads=[b_nmT])

                    def conv_proj(tt):
                        cur = tt % 2
                        for ch in range(2):
                            pv_, bpv_ = proj_fm(tt, C_AV + ch * 128)
                            pg_, bpg_ = proj_fm(tt, C_AG + ch * 128)
                            S.op("act", lambda e, ch=ch, pg_=pg_: e.activation(th[ch][:], pg_[:, :], AF.Tanh, scale=0.5),
                                 reads=[bpg_], writes=[b_th[ch]])
                            S.op("dve", lambda e, ch=ch, pv_=pv_: e.scalar_tensor_tensor(
                                hglu[:, cur, ch, 32:544], th[ch][:], 1.0, pv_[:, :], ALU.add, ALU.mult),
                                reads=[b_th[ch], bpv_], writes=[b_hglu[cur]])

                    def pad_copy(tt):
                        cur = tt % 2
                        if tt + 1 < NT:
                            S.op("pool", lambda e: e.tensor_copy(hglu[:, 1 - cur, :, 0:32], hglu[:, cur, :, 512:544]),
                                 reads=[b_hglu[cur]], writes=[b_hglu[1 - cur]])

                    def v_proj(tt):
                        hTc, bhTc = hT[tt % 2], b_hT[tt % 2]
                        t0 = tt * 512
                        for j in range(4):
                            pv1, bpv1 = psum("pj")
                            pv2, bpv2 = psum("pj")

                            def fv(e, j=j, pv1=pv1, pv2=pv2):
                                ins = None
                                for c in range(NCH):
                                    e.matmul(pv1[:, 0:384], hTc[:, c, j * 128:(j + 1) * 128], wb[:, c, C_VB:C_VB + 384],
                                             start=(c == 0), stop=(c == NCH - 1))
                                for c in range(NCH):
                                    ins = e.matmul(pv2[:, 0:390], hTc[:, c, j * 128:(j + 1) * 128], wb[:, c, C_VC:C_VC + 390],
                                                   start=(c == 0), stop=(c == NCH - 1))
                                return ins
                            S.op("pe", fv, reads=[bhTc, wbuf(C_VB), wbuf(C_VC)], writes=[bpv1, bpv2])
                            vj = vst[j % 2]
                            bvj = b_vst[j % 2]
                            S.op("act", lambda e, vj=vj, pv1=pv1: e.activation(vj[:, 0:384], pv1[:, 0:384], AF.Copy),
                                 reads=[bpv1], writes=[bvj])
                            S.op("act", lambda e, vj=vj, pv2=pv2: e.activation(vj[:, 384:768], pv2[:, 0:384], AF.Copy),
                                 reads=[bpv2], writes=[bvj])
                            jj = tt * 4 + j
                            S.op("act", lambda e, jj=jj, pv2=pv2: e.activation(fl[:, jj, :], pv2[:, 384:390], AF.Copy),
                                 reads=[bpv2], writes=[b_fl])
                            S.dma("sp", "vst%d" % (j % 2), vs[t0 + j * 128:t0 + (j + 1) * 128, :], vj[:, :], reads=[bvj])

                    lnst = {}

                    def conv_mm(tt):
                        cur = tt % 2
                        for ch in range(2):
                            pcv, bpcv = psum("pj")

                            def fcv(e, ch=ch, pcv=pcv):
                                ins = None
                                for k_ in range(CONVW):
                                    o0 = 2 + k_
                                    ins = e.matmul(pcv[:, :], diag[:, ch, k_, :], hglu[:, cur, ch, o0:o0 + 512],
                                                   start=(k_ == 0), stop=(k_ == CONVW - 1))
                                return ins
                            S.op("pe", fcv, reads=[b_diag, b_hglu[cur]], writes=[bpcv])
                            cbc = 32 + l * 2 + ch
                            S.op("act", lambda e, ch=ch, pcv=pcv, cbc=cbc: e.activation(
                                ycv[ch][:], pcv[:, :], AF.Identity, bias=prmT[:, cbc:cbc + 1]),
                                reads=[bpcv, b_const], writes=[b_ycv[ch]])
                            S.op("act", lambda e, ch=ch, pcv=pcv, cbc=cbc: e.activation(
                                ysq[ch][:], pcv[:, :], AF.Square, bias=prmT[:, cbc:cbc + 1]),
                                reads=[bpcv, b_const], writes=[b_ysq[ch]])
                        pst, bpst = psum("pj")

                        def fst(e):
                            ins = None
                            first = True
                            for j in range(4):
                                for w_, src in enumerate((ycv, ysq)):
                                    for ch in range(2):
                                        ins = e.matmul(pst[:, j * 2 + w_:j * 2 + w_ + 1], src[ch][:, j * 128:(j + 1) * 128], onescol[:, :],
                                                       start=first, stop=(j == 3 and w_ == 1 and ch == 1), skip_group_check=True)
                                        first = False
                            return ins
                        S.op("pe", fst, reads=b_ycv + b_ysq + [b_sm], writes=[bpst])
                        S.op("dve", lambda e: e.tensor_scalar(lns["s"][:], pst[:, 0:8], 1.0 / 256, None, ALU.mult),
                             reads=[bpst], writes=[b_lns["s"]])

                    def ln_chain(tt):
                        t0 = tt * 512
                        L_, bL = lns, b_lns
                        s3 = L_["s"][:].rearrange("p (j w) -> p j w", w=2)
                        v4 = L_["v"][:, 0:4]
                        S.op("dve", lambda e: e.tensor_tensor(L_["t1"][:, 0:4], s3[:, :, 0], s3[:, :, 0], ALU.mult), reads=[bL["s"]], writes=[bL["t1"]])
                        S.op("dve", lambda e: e.tensor_tensor(v4, s3[:, :, 1], L_["t1"][:, 0:4], ALU.subtract), reads=[bL["s"], bL["t1"]], writes=[bL["v"]])
                        S.op("dve", lambda e: e.tensor_scalar(v4, v4, LN_EPS, None, ALU.add), reads=[bL["v"]], writes=[bL["v"]])
                        rsqrt_newton(v4, L_["y"][:, 0:4], L_["ti"][:, 0:4], L_["t1"][:, 0:4], bL["v"], bL["y"], bL["ti"], bL["t1"], iters=3)
                        ab3 = L_["ab"][:].rearrange("p (j w) -> p j w", w=2)
                        S.op("dve", lambda e: e.tensor_copy(ab3[:, :, 0], L_["y"][:, 0:4]), reads=[bL["y"]], writes=[bL["ab"]])
                        S.op("dve", lambda e: e.tensor_tensor(ab3[:, :, 1], s3[:, :, 0], L_["y"][:, 0:4], ALU.mult), reads=[bL["s"], bL["y"]], writes=[bL["ab"]])
                        S.op("dve", lambda e: e.tensor_copy(L_["hi"][:], L_["ab"][:]), reads=[bL["ab"]], writes=[bL["hi"]])
                        S.op("dve", lambda e: e.tensor_copy(L_["abf"][:], L_["hi"][:]), reads=[bL["hi"]], writes=[bL["abf"]])
                        S.op("dve", lambda e: e.tensor_tensor(L_["lo"][:], L_["ab"][:], L_["abf"][:], ALU.subtract), reads=[bL["ab"], bL["abf"]], writes=[bL["lo"]])

                    def ln_pe(tt):
                        t0 = tt * 512
                        L_, bL = lns, b_lns
                        ptr, bptr = psum("tp")
                        ptrb = ptr[:].bitcast(BF16)

                        def ftr(e):
                            ins = None
                            for w_, key in enumerate(("hi", "lo")):
                                for j in range(4):
                                    ins = e.transpose(ptrb[0:2, w_ * 512 + j * 128:w_ * 512 + (j + 1) * 128],
                                                      L_[key][:, j * 2:(j + 1) * 2], identb[:, :])
                            return ins
                        S.op("pe", ftr, reads=[bL["hi"], bL["lo"], b_const], writes=[bptr])
                        S.op("act", lambda e: e.activation(abT[:, :], ptrb[0:2, :], AF.Copy), reads=[bptr], writes=[b_abT])
                        pA, bpA = psum("pj")
                        pB, bpB = psum("pj")

                        def fab(e):
                            e.matmul(pA[:, :], selab[:, 0, :], abT[:, 0:512], start=True, stop=False)
                            e.matmul(pA[:, :], selab[:, 0, :], abT[:, 512:1024], start=False, stop=True)
                            e.matmul(pB[:, :], selab[:, 1, :], abT[:, 0:512], start=True, stop=False)
                            return e.matmul(pB[:, :], selab[:, 1, :], abT[:, 512:1024], start=False, stop=True)
                        S.op("pe", fab, reads=[b_abT, b_sm], writes=[bpA, bpB])
                        for ch in range(2):
                            S.op("dve", lambda e, ch=ch: e.tensor_tensor(ycv[ch][:], ycv[ch][:], pA[:, :], ALU.mult),
                                 reads=[b_ycv[ch], bpA], writes=[b_ycv[ch]])
                            S.op("dve", lambda e, ch=ch: e.tensor_tensor(ycv[ch][:], ycv[ch][:], pB[:, :], ALU.subtract),
                                 reads=[b_ycv[ch], bpB], writes=[b_ycv[ch]])
                            gc_ = 36 + l * 2 + ch
                            bc_ = 40 + l * 2 + ch
                            S.op("act", lambda e, ch=ch, gc_=gc_, bc_=bc_: e.activation(
                                mixA[:, ch, t0:t0 + 512], ycv[ch][:], AF.Silu,
                                scale=prmT[:, gc_:gc_ + 1], bias=prmT[:, bc_:bc_ + 1]),
                                reads=[b_ycv[ch], b_const], writes=[b_mixA])

                    load_x(0)
                    s1_part1(0)
                    s1_part2(0)
                    for gi_, (c0_, c1_) in enumerate(wgrp):
                        S.op("dve", lambda e, c0_=c0_, c1_=c1_: e.tensor_tensor(
                            wb[:, :, c0_:c1_], wb[:, :, c0_:c1_],
                            prmT[:, pc:pc + NCH].unsqueeze(2).to_broadcast([128, NCH, c1_ - c0_]), ALU.mult),
                            reads=[b_wg[gi_], b_const], writes=[b_wg[gi_]])
                    for tt in range(NT + 2):
                        prod = tt < NT
                        cons = 0 < tt <= NT
                        if prod:
                            qk_proj(tt, 0)
                        if tt + 1 < NT and tt < NT:
                            s1_part1(tt + 1)
                        if tt >= 2:
                            ln_pe(tt - 2)
                        if cons:
                            qk_chain1(tt - 1, 0)
                            qk_chain1(tt - 1, 1)
                        if prod:
                            qk_proj(tt, 1)
                        if cons:
                            qk_chain2(tt - 1, 0)
                            gate(tt - 1)
                        if prod:
                            conv_proj(tt)
                        if cons:
                            qk_chain2(tt - 1, 1)
                            conv_mm(tt - 1)
                        if prod:
                            pad_copy(tt)
                            v_proj(tt)
                        if cons:
                            gate2(tt - 1)
                            ln_chain(tt - 1)
                        if tt + 1 < NT:
                            s1_part2(tt + 1)

                    with ExitStack() as esf:
                        W = NJ * NH

                        def v3(t_):
                            ap_ = t_[:] if t_[:].dtype == F32 else t_[:].bitcast(F32)
                            return ap_[:, 0:W].rearrange("p (j h) -> p j h", h=NH)
                        z, ez, sp_, tot, inc, hif = [v3(qraw[0][0][i_]) for i_ in range(6)]
                        r1 = v3(ycv[0])
                        c3 = ycv[1][:].bitcast(BF16)[:, 0:NH * 3 * NJ].rearrange("p (h r j) -> p h r j", h=NH, r=3)
                        cT = sqb[0]
                        b_z, b_ez, b_sp, b_tot, b_inc, b_hif = b_qraw[0][0]
                        b_r1, b_c3, b_cT = b_ycv[0], b_ycv[1], b_sq[0]
                        S.op("dve", lambda e: e.tensor_tensor(z[:], fl[:], bfb[:, l, :].unsqueeze(1).to_broadcast([128, NJ, NH]), ALU.add),
                             reads=[b_fl, b_const], writes=[b_z])
                        S.op("act", lambda e: e.activation(ez[:], z[:], AF.Exp, scale=-1.0), reads=[b_z], writes=[b_ez])
                        S.op("act", lambda e: e.activation(sp_[:], ez[:], AF.Ln, bias=1.0), reads=[b_ez], writes=[b_sp])
                        pw, bpw = psum("pj")
                        pt_, bpt_ = psum("pj")
                        spf = sp_[:].rearrange("p j h -> p (j h)")
                        S.op("pe", lambda e: e.matmul(pw[:, 0:W], tri[:, :], spf, start=True, stop=True), reads=[b_sp, b_const], writes=[bpw])
                        S.op("pe", lambda e: e.matmul(pt_[:, 0:W], onesf[:, :], spf, start=True, stop=True), reads=[b_sp, b_const], writes=[bpt_])
                        S.op("act", lambda e: e.activation(tot[:].rearrange("p j h -> p (j h)"), pt_[:, 0:W], AF.Copy), reads=[bpt_], writes=[b_tot])
                        for h in range(NH):
                            S.op("dve", lambda e, h=h: e.tensor_tensor_scan(inc[:, :, h], ones_row[:, 0:NJ], tot[:, :, h], 0.0, ALU.mult, ALU.add),
                                 reads=[b_tot, b_const], writes=[b_inc])
                        S.op("dve", lambda e: e.tensor_tensor(inc[:], inc[:], tot[:], ALU.subtract), reads=[b_inc, b_tot], writes=[b_inc])
                        S.op("dve", lambda e: e.tensor_tensor(negc[:].rearrange("p j h -> p (j h)"), pw[:, 0:W], inc[:].rearrange("p j h -> p (j h)"), ALU.add),
                             reads=[bpw, b_inc], writes=[b_negc])
                        c3v = [c3[:, :, r_, :].rearrange("p h j -> p j h") for r_ in range(3)]
                        S.op("dve", lambda e: e.tensor_scalar(c3v[0], negc[:], -1.0, None, ALU.mult), reads=[b_negc], writes=[b_c3])
                        S.op("dve", lambda e: e.tensor_copy(hif[:], c3v[0]), reads=[b_c3], writes=[b_hif])
                        S.op("dve", lambda e: e.scalar_tensor_tensor(r1[:], negc[:], -1.0, hif[:], ALU.mult, ALU.subtract),
                             reads=[b_negc, b_hif], writes=[b_r1])
                        S.op("dve", lambda e: e.tensor_copy(c3v[1], r1[:]), reads=[b_r1], writes=[b_c3])
                        S.op("dve", lambda e: e.tensor_copy(hif[:], c3v[1]), reads=[b_c3], writes=[b_hif])
                        S.op("dve", lambda e: e.tensor_tensor(r1[:], r1[:], hif[:], ALU.subtract), reads=[b_r1, b_hif], writes=[b_r1])
                        S.op("dve", lambda e: e.tensor_copy(c3v[2], r1[:]), reads=[b_r1], writes=[b_c3])
                        X = NH * 3 * NJ
                        c3f = c3[:].rearrange("p h r j -> p (h r j)")
                        for g0 in range(0, X, 128):
                            n_ = min(128, X - g0)
                            ptc, bptc = psum("tp")
                            ptcb = ptc[:].bitcast(BF16)
                            S.op("pe", lambda e, g0=g0, n_=n_, ptcb=ptcb: e.transpose(ptcb[0:n_, 0:128], c3f[:, g0:g0 + n_], identb[:, :]),
                                 reads=[b_c3, b_const], writes=[bptc])
                            S.op("act", lambda e, n_=n_, ptcb=ptcb: e.activation(cT[0:n_, 0:128], ptcb[0:n_, 0:128], AF.Copy), reads=[bptc], writes=[b_cT])
                            S.dma("sp", "cT", cf[g0:g0 + n_, :], cT[0:n_, 0:128], reads=[b_cT])
                S.barrier()

            _nm[0] += 1
            wd_cm = nc.sbuf_tensor("wd_%d" % _nm[0], [128, NFC, D], BF16)
            wd = wd_cm.__enter__()
            b_wd = S.buf("wd")
            if "D" in phases:
                S.dma("pool", "wd", wd[:, :, :], P["w_down"][l].rearrange("(f p) d -> p f d", p=128), writes=[b_wd],
                      max_dma_last_dim=4096)
            _nm[0] += 1
            wo_cm = nc.sbuf_tensor("wo_%d" % _nm[0], [128, NCH, D], BF16)
            wo = wo_cm.__enter__()
            b_wo = S.bufs(NCH, "wo")
            if "C" in phases:
                for c in range(NCH):
                    S.dma("pool", "wb%d" % c, wo[:, c, :], P["w_out"][l, c * 128:(c + 1) * 128, :], writes=[b_wo[c]],
                          max_dma_last_dim=4096)
            with ExitStack() as es:
                mixB = sb(es, "mixB", (128, 6, T), BF16)
                b_mixB = S.buf("mixB")
                if "B" in phases:
                    with ExitStack() as esb:
                        set_pools(s=[0, 1, 2, 3, 4], o=[5, 6, 7])
                        qa = [sb(esb, "qa%d" % i, (96, T), BF16) for i in range(2)]
                        ka = [sb(esb, "ka%d" % i, (96, T), BF16) for i in range(2)]
                        va = [sb(esb, "va%d" % i, (128, NJ, 128), BF16) for i in range(2)]
                        b_qa, b_ka, b_va = S.bufs(2, "qa"), S.bufs(2, "ka"), S.bufs(2, "va")
                        pb = [sb(esb, "pb%d" % i, (128, 512), BF16) for i in range(4)]
                        b_pb = S.bufs(4, "pb")
                        rcs = [sb(esb, "rcs%d" % i, (64, 512), F32) for i in range(2)]
                        b_rcs = S.bufs(2, "rcs")
                        for i in range(2):
                            S.op("pool", lambda e, i=i: e.memset(va[i][:, :, 64:128], 1.0), writes=[b_va[i]])

                        def load_head(hh):
                            i = hh % 2
                            moba = hh < NH
                            h = hh % NH
                            S.dma("sp", "qa%d" % i, qa[i][0:64, :], (qm if moba else qf)[h], writes=[b_qa[i]])
                            S.dma("sp", "ka%d" % i, ka[i][0:64, :], (km if moba else kf)[h], writes=[b_ka[i]])
                            if moba:
                                S.dma("sp", "qa%d" % i, qa[i][64:80, :], nm[h], writes=[b_qa[i]])
                                S.dma("sp", "ka%d" % i, ka[i][64:80, :], C["blkoh"], writes=[b_ka[i]])
                            else:
                                S.dma("sp", "qa%d" % i, qa[i][64:67, :], cf[h * 3 * NJ:(h + 1) * 3 * NJ, :].rearrange("(r j) p -> r (j p)", r=3),
                                      writes=[b_qa[i]])
                                S.dma("sp", "ka%d" % i, ka[i][64:67, :], C["ones3"], writes=[b_ka[i]])
                            vcol = (0 if moba else 384) + h * 64
                            S.dma("sp", "va%d" % i, va[i][:, :, 0:64], vs[:, vcol:vcol + 64].rearrange("(j p) d -> p j d", p=128),
                                  writes=[b_va[i]])

                        steps = []
                        for hh in range(2 * NH):
                            moba = hh < NH
                            if moba:
                                for blk in range(NB):
                                    for n in range(blk + 1):
                                        steps.append(dict(hh=hh, q=blk, k=n, first=(n == 0), last=(n == blk)))
                            else:
                                for qt in range(NT):
                                    for j in range(4 * qt + 4):
                                        steps.append(dict(hh=hh, q=qt, k=j, first=(j == 0), last=(j == 4 * qt + 3)))

                        load_head(0)
                        state = {"O": None, "bO": None, "pbi": 0, "rci": 0}

                        def emit_qk(st):
                            hh = st["hh"]
                            i = hh % 2
                            moba = hh < NH
                            h = hh % NH
                            Sp, bSp = psum("s")
                            st["Sp"], st["bSp"] = Sp, bSp
                            if moba:
                                blk, n = st["q"], st["k"]
                                q0 = blk * 256

                                def f(e):
                                    ins = None
                                    for u in range(2):
                                        j = 2 * n + u
                                        o = Sp[:, u * 256:(u + 1) * 256]
                                        if n < blk:
                                            near = (j == 2 * blk - 1)
                                            ins = e.matmul(o, ka[i][0:80, j * 128:(j + 1) * 128], qa[i][0:80, q0:q0 + 256],
                                                           start=True, stop=not near)
                                            if near:
                                                ins = e.matmul(o, antij[:, :], gt[:, h, 256:512], start=False, stop=True)
                                        else:
                                            c0 = 128 if u == 0 else 0
                                            e.matmul(o, ka[i][0:64, j * 128:(j + 1) * 128], qa[i][0:64, q0:q0 + 256],
                                                     start=True, stop=False)
                                            ins = e.matmul(o, antij[:, :], gt[:, h, c0:c0 + 256], start=False, stop=True)
                                    return ins
                            else:
                                qt, j = st["q"], st["k"]
                                q0 = qt * 512

                                cl = max(0, j - 4 * qt) * 128
                                st["cl"] = cl

                                def f(e):
                                    diagt = j >= 4 * qt
                                    ins = e.matmul(Sp[:, cl:512], ka[i][0:67, j * 128:(j + 1) * 128], qa[i][0:67, q0 + cl:q0 + 512],
                                                   start=True, stop=not diagt)
                                    if diagt:
                                        c0 = 384 - (j - 4 * qt) * 128
                                        ins = e.matmul(Sp[:, cl:512], antij[:, :], gm[:, c0 + cl:c0 + 512], start=False, stop=True)
                                    return ins
                            S.op("pe", f, reads=[b_qa[i], b_ka[i], b_const, b_gt], writes=[bSp])

                        def emit_exp(st):
                            hh = st["hh"]
                            moba = hh < NH
                            h = hh % NH
                            k_ = state["pbi"] % 4
                            state["pbi"] += 1
                            st["pb"], st["bpb"] = pb[k_], b_pb[k_]
                            Sp = st["Sp"]
                            if moba:
                                S.op("act", lambda e: e.activation(pb[k_][:], Sp[:, :], AF.Exp), reads=[st["bSp"]], writes=[b_pb[k_]])
                            else:
                                j = st["k"]
                                cl = st["cl"]
                                S.op("act", lambda e: e.activation(pb[k_][:, cl:512], Sp[:, cl:512], AF.Exp, bias=negc[:, j, h:h + 1]),
                                     reads=[st["bSp"], b_negc], writes=[b_pb[k_]])

                        def emit_pv(st):
                            hh = st["hh"]
                            i = hh % 2
                            moba = hh < NH
                            h = hh % NH
                            if st["first"]:
                                state["O"], state["bO"] = psum("o")
                            O, bO = state["O"], state["bO"]
                            pbt = st["pb"]
                            if moba:
                                n = st["k"]

                                def f(e):
                                    e.matmul(O[:, 0:256], va[i][:, 2 * n, :], pbt[:, 0:256], start=st["first"], stop=False)
                                    return e.matmul(O[:, 0:256], va[i][:, 2 * n + 1, :], pbt[:, 256:512], start=False, stop=st["last"])
                                NQ = 256
                                q0 = st["q"] * 256
                            else:
                                j = st["k"]

                                cl = st["cl"]

                                def f(e):
                                    return e.matmul(O[:, cl:512], va[i][:, j, :], pbt[:, cl:512], start=st["first"], stop=st["last"])
                                NQ = 512
                                q0 = st["q"] * 512
                            S.op("pe", f, reads=[b_va[i], st["bpb"]], writes=[bO])
                            if st["last"]:
                                r_ = state["rci"] % 2
                                state["rci"] += 1
                                S.op("dve", lambda e: e.reciprocal(rcs[r_][0:64, 0:NQ], O[64:128, 0:NQ]), reads=[bO], writes=[b_rcs[r_]])
                                chunk = (0 if moba else 3) + h // 2
                                d0 = (h % 2) * 64
                                S.op("dve", lambda e: e.tensor_tensor(mixB[d0:d0 + 64, chunk, q0:q0 + NQ], O[0:64, 0:NQ], rcs[r_][0:64, 0:NQ], ALU.mult),
                                     reads=[bO, b_rcs[r_]], writes=[b_mixB])

                        nsteps = len(steps)
                        LA = 2
                        for si in range(min(LA, nsteps)):
                            emit_qk(steps[si])
                        for si in range(nsteps):
                            st = steps[si]
                            if st["first"] and st["q"] == 0 and st["hh"] + 1 < 2 * NH:
                                load_head(st["hh"] + 1)
                            if si + LA < nsteps:
                                emit_qk(steps[si + LA])
                            emit_exp(st)
                            emit_pv(st)
                    S.barrier()

                if "C" in phases:
                    with ExitStack() as esc:
                        set_pools(o=[0, 1, 2, 3])
                        xs = [sb(esc, "xc%d" % i, (128, 4, D), F32) for i in range(2)]
                        b_xs = S.bufs(2, "xc")
                        S.dma("sp", "xs0", xs[0][:], x_src[0:512, :].rearrange("(j p) d -> p j d", p=128), writes=[b_xs[0]])
                        for tt in range(NT):
                            cur = tt % 2
                            t0 = tt * 512
                            if tt + 1 < NT:
                                nx = (tt + 1) % 2
                                S.dma("sp", "xs%d" % nx, xs[nx][:], x_src[t0 + 512:t0 + 1024, :].rearrange("(j p) d -> p j d", p=128),
                                      writes=[b_xs[nx]])
                            for j in range(4):
                                for hf in range(2):
                                    po, bpo = psum("o")

                                    def f(e, j=j, hf=hf, po=po):
                                        ins = None
                                        for c in range(NCH):
                                            lhs = (mixA[:, c, t0 + j * 128:t0 + (j + 1) * 128] if c < 2
                                                   else mixB[:, c - 2, t0 + j * 128:t0 + (j + 1) * 128])
                                            ins = e.matmul(po[:, :], lhs, wo[:, c, hf * 512:(hf + 1) * 512],
                                                           start=(c == 0), stop=(c == NCH - 1))
                                        return ins
                                    S.op("pe", f, reads=[b_mixA, b_mixB] + b_wo, writes=[bpo])
                                    S.op("dve", lambda e, j=j, hf=hf, po=po, cur=cur: e.tensor_tensor(
                                        xs[cur][:, j, hf * 512:(hf + 1) * 512], po[:, :], xs[cur][:, j, hf * 512:(hf + 1) * 512], ALU.add),
                                        reads=[bpo, b_xs[cur]], writes=[b_xs[cur]])
                            S.dma("sp", "xs%d" % cur, xa[t0:t0 + 512, :].rearrange("(j p) d -> p j d", p=128), xs[cur][:], reads=[b_xs[cur]])
                    S.barrier()

            wo_cm.__exit__(None, None, None)
            if "D" in phases:
                with ExitStack() as esd:
                    set_pools(acc=[0, 1, 2, 3], gu=[4, 5], tp=[6, 7])
                    wgu = sb(esd, "wgu", (128, NCH, 2 * DFF), BF16)
                    pc2 = 16 + l * NCH
                    NG = NFC // 2
                    b_wf = S.bufs(NG, "wf")
                    wgu_v = P["w_gate_up"][l].rearrange("(c p) f -> p c f", p=128)
                    wd_v = P["w_down"][l].rearrange("(f p) d -> p f d", p=128)
                    for g_ in range(NG):
                        S.dma("pool", "wf%d" % g_, wgu[:, :, g_ * 256:(g_ + 1) * 256], wgu_v[:, :, g_ * 256:(g_ + 1) * 256],
                              writes=[b_wf[g_]], max_dma_last_dim=4096)
                        S.dma("pool", "wf%d" % g_, wgu[:, :, DFF + g_ * 256:DFF + (g_ + 1) * 256],
                              wgu_v[:, :, DFF + g_ * 256:DFF + (g_ + 1) * 256], writes=[b_wf[g_]], max_dma_last_dim=4096)

                    def fold_group(g_):
                        for o_ in (0, DFF):
                            S.op("dve", lambda e, o_=o_: e.tensor_tensor(
                                wgu[:, :, o_ + g_ * 256:o_ + (g_ + 1) * 256], wgu[:, :, o_ + g_ * 256:o_ + (g_ + 1) * 256],
                                prmT[:, pc2:pc2 + NCH].unsqueeze(2).to_broadcast([128, NCH, 256]), ALU.mult),
                                reads=[b_wf[g_], b_const], writes=[b_wf[g_]])
                    NT2 = T // 256
                    xs = [sb(esd, "xd%d" % i, (128, 2, D), F32) for i in range(3)]
                    b_xs = S.bufs(3, "xd")
                    hT = [sb(esd, "hD%d" % i, (128, NCH, 256), BF16) for i in range(2)]
                    b_hT = S.bufs(2, "hD")
                    tmp = make_norm_tmp(esd, 2)
                    sg = [sb(esd, "sg%d" % i, (128, 256), BF16) for i in range(3)]
                    b_sg = S.bufs(3, "sg")
                    aa = [sb(esd, "aa%d" % i, (128, 256), BF16) for i in range(5)]
                    b_aa = S.bufs(5, "aa")

                    def load_xd(tt):
                        k_ = tt % 3
                        S.dma("sp", "xs%d" % k_, xs[k_][:], xa[tt * 256:(tt + 1) * 256, :].rearrange("(j p) d -> p j d", p=128),
                              writes=[b_xs[k_]])

                    def n1(tt):
                        k_ = tt % 3
                        norm_part1([xs[k_][:, 0, :], xs[k_][:, 1, :]], [b_xs[k_], b_xs[k_]], 2, tmp)

                    def n2(tt):
                        norm_part2(2, hT[tt % 2], b_hT[tt % 2], tmp)

                    load_xd(0)
                    if NT2 > 1:
                        load_xd(1)
                    n1(0)
                    n2(0)
                    gi = 0
                    for tt in range(NT2):
                        cur = tt % 3
                        t0 = tt * 256
                        hTc, bhTc = hT[tt % 2], b_hT[tt % 2]
                        acc = [psum("acc") for _ in range(4)]

                        def emit_gu(fc, hTc=hTc, bhTc=bhTc):
                            pg, bpg = psum("gu")

                            def f(e):
                                for c in range(NCH):
                                    e.matmul(pg[:, 0:256], wgu[:, c, fc * 128:(fc + 1) * 128], hTc[:, c, :],
                                             start=(c == 0), stop=(c == NCH - 1))
                                ins = None
                                for c in range(NCH):
                                    ins = e.matmul(pg[:, 256:512], wgu[:, c, DFF + fc * 128:DFF + (fc + 1) * 128], hTc[:, c, :],
                                                   start=(c == 0), stop=(c == NCH - 1))
                                return ins
                            S.op("pe", f, reads=[bhTc, b_wf[fc // 2]], writes=[bpg])
                            return pg, bpg

                        if tt == 0:
                            fold_group(0)
                        nxt = emit_gu(0)
                        for fc in range(NFC):
                            pg, bpg = nxt
                            if fc + 1 < NFC:
                                if tt == 0 and (fc + 1) % 2 == 0:
                                    fold_group((fc + 1) // 2)
                                nxt = emit_gu(fc + 1)
                            if fc == 3 and tt + 1 < NT2:
                                n1(tt + 1)
                                if tt + 2 < NT2:
                                    load_xd(tt + 2)
                            if fc == 12 and tt + 1 < NT2:
                                n2(tt + 1)
                            sgi = sg[gi % 3]
                            bsgi = b_sg[gi % 3]
                            aai = aa[gi % 5]
                            baai = b_aa[gi % 5]
                            gi += 1
                            S.op("act", lambda e: e.activation(sgi[:], pg[:, 0:256], AF.Silu), reads=[bpg], writes=[bsgi])
                            S.op("dve", lambda e: e.tensor_tensor(aai[:], sgi[:], pg[:, 256:512], ALU.mult),
                                 reads=[bsgi, bpg], writes=[baai])

                            def fd(e):
                                ins = None
                                for j in range(2):
                                    for hf in range(2):
                                        ins = e.matmul(acc[j * 2 + hf][0][:, :], aai[:, j * 128:(j + 1) * 128],
                                                       wd[:, fc, hf * 512:(hf + 1) * 512], start=(fc == 0), stop=(fc == NFC - 1))
                                return ins
                            S.op("pe", fd, reads=[baai, b_wd], writes=[a_[1] for a_ in acc])
                        for j in range(2):
                            for hf in range(2):
                                a_, ba_ = acc[j * 2 + hf]
                                S.op("dve", lambda e: e.tensor_tensor(
                                    xs[cur][:, j, hf * 512:(hf + 1) * 512], a_[:, :], xs[cur][:, j, hf * 512:(hf + 1) * 512], ALU.add),
                                    reads=[ba_, b_xs[cur]], writes=[b_xs[cur]])
                        S.dma("sp", "xs%d" % cur, x_dst[t0:t0 + 256, :].rearrange("(j p) d -> p j d", p=128), xs[cur][:], reads=[b_xs[cur]])
                S.barrier()
            wd_cm.__exit__(None, None, None)

        S.barrier()
        S.emit()
    return nc


_CACHE = {}


def kernel(**inputs):
    T = 4096
    x = np.ascontiguousarray(np.asarray(inputs["x"], dtype=np.float32))
    B = x.shape[0]
    if "nc" not in _CACHE:
        _CACHE["nc"] = build_program(T=T, depth=2)
        _CACHE["consts"] = make_consts(T)
    nc = _CACHE["nc"]
    consts = _CACHE["consts"]
    base = {n: np.ascontiguousarray(np.asarray(inputs[n], dtype=np.float32)) for n, _ in PARAM_SPECS}
    for k, v in consts.items():
        base["c_" + k] = v
    in_maps = []
    for b in range(B):
        m = dict(base)
        m["x"] = x[b]
        in_maps.append(m)
    res = run_bass_kernel_spmd(nc, in_maps, core_ids=list(range(B)))
    return np.stack([np.asarray(r["out"], dtype=np.float32) for r in res.results], axis=0)
```

```python
import math
from contextlib import ExitStack

import numpy as np
import ml_dtypes
import concourse.bass as bass
import concourse.mybir as mybir
from concourse.ap import AP
from concourse.bass_utils import run_bass_kernel_spmd

F32 = mybir.dt.float32
BF16 = mybir.dt.bfloat16
I32 = mybir.dt.int32
ALU = mybir.AluOpType
AF = mybir.ActivationFunctionType
AX = mybir.AxisListType

D = 1024
NCH = D // 128
IN_DIM = 2822
DFF = 2816
NFC = DFF // 128
CONVW = 31
NH = 6
NEGM = -30000.0
RMS_EPS = 1e-6
LN_EPS = 1e-5
C_AV, C_AG, C_QB, C_KB, C_VB, C_QC, C_KC, C_VC, C_F = 0, 256, 512, 896, 1280, 1664, 2048, 2432, 2816

ENGS = ("pe", "act", "dve", "pool", "sp")


class Buf:
    __slots__ = ("name", "w", "r")

    def __init__(self, name):
        self.name = name
        self.w = None
        self.r = []


class Sched:
    def __init__(self, nc):
        self.nc = nc
        self.eobj = {"pe": nc.tensor, "act": nc.scalar, "dve": nc.vector, "pool": nc.gpsimd, "sp": nc.sync}
        self.sem = {}
        self.cnt = {e: 0 for e in ENGS}
        for e in ("pe", "act", "dve", "pool"):
            self.sem[e] = nc.alloc_semaphore(name="sem_" + e)
        self.waited = {}
        self.dma_sems = {}
        self.nbuf = 0
        import os as _os
        self.maxops = int(_os.environ.get("MK_MAXOPS", "0")) or None
        self.nops = 0
        self.oplog = []

    def buf(self, name=None):
        self.nbuf += 1
        return Buf(name or ("b%d" % self.nbuf))

    def bufs(self, n, name="b"):
        return [self.buf("%s%d" % (name, i)) for i in range(n)]

    def _deps(self, reads, writes):
        deps = []
        for b in reads:
            if b.w is not None:
                deps.append(b.w)
        for b in writes:
            if b.w is not None:
                deps.append(b.w)
            deps.extend(b.r)
        return deps

    def _emit_waits(self, eng, deps, skip_self=False):
        need = {}
        for (kind, key, idx) in deps:
            if skip_self and kind == "eng" and key == eng:
                continue
            k = (kind, key)
            if need.get(k, -1) < idx:
                need[k] = idx
        for (kind, key), idx in need.items():
            if kind == "eng":
                val = idx + 1
                sem = self.sem[key]
            else:
                val = idx
                sem = self.dma_sems[key][0]
            wk = (eng, kind, key)
            if self.waited.get(wk, 0) >= val:
                continue
            self.waited[wk] = val
            self.eobj[eng].wait_ge(sem, val)

    def _mark(self, reads, writes, tag):
        for b in reads:
            b.r.append(tag)
            if len(b.r) > 64:
                b.r = self._compact(b.r)
        for b in writes:
            b.w = tag
            b.r = []

    @staticmethod
    def _compact(tags):
        best = {}
        for (kind, key, idx) in tags:
            k = (kind, key)
            if best.get(k, -1) < idx:
                best[k] = idx
        return [(k[0], k[1], v) for k, v in best.items()]

    def _skip(self, what):
        self.nops += 1
        if self.maxops is None:
            return False
        import sys as _sys
        fr = _sys._getframe(2)
        self.oplog.append((self.nops, what, fr.f_lineno))
        return self.nops > self.maxops

    def op(self, eng, fn, reads=(), writes=()):
        if self._skip(eng):
            return
        reads = [b for b in reads if b is not None]
        writes = [b for b in writes if b is not None]
        self._emit_waits(eng, self._deps(reads, writes), skip_self=(eng == "pe"))
        idx = self.cnt[eng]
        self.cnt[eng] += 1
        sem = self.sem[eng]
        fn(self.eobj[eng]).then_inc(sem, 1)
        self._mark(reads, writes, ("eng", eng, idx))

    def dma(self, queue, slot, out, in_, reads=(), writes=(), **kw):
        if self._skip("dma:" + queue):
            return
        reads = [b for b in reads if b is not None]
        writes = [b for b in writes if b is not None]
        if slot not in self.dma_sems:
            self.dma_sems[slot] = [self.nc.alloc_semaphore(name="dsem_" + slot), 0]
        ent = self.dma_sems[slot]
        self._emit_waits(queue, self._deps(reads, writes))
        ent[1] += 16
        val = ent[1]
        sem = ent[0]
        self.eobj[queue].dma_start(out=out, in_=in_, **kw).then_inc(sem, 16)
        self._mark(reads, writes, ("dma", slot, val))

    def barrier(self):
        deps = [("eng", e, self.cnt[e] - 1) for e in ("pe", "act", "dve", "pool") if self.cnt[e] > 0]
        deps += [("dma", s, ent[1]) for s, ent in self.dma_sems.items() if ent[1] > 0]
        for e in ENGS:
            self._emit_waits(e, deps, skip_self=False)

    def emit(self):
        if self.maxops is not None:
            for o in self.oplog[-6:] if self.maxops >= self.nops else self.oplog[max(0, self.maxops - 3):self.maxops + 2]:
                print("OPLOG", o)
            print("OPLOG total", self.nops)
        return


def _t5_bucket(n):
    n = np.maximum(n, 0)
    nf = np.maximum(n, 1).astype(np.float32)
    large = 16 + (np.log(nf / np.float32(16)) / np.float32(math.log(128 / 16)) * np.float32(16)).astype(np.int32)
    large = np.minimum(large, 31)
    return np.where(n < 16, n, large)


def make_consts(T):
    bf = ml_dtypes.bfloat16
    c = {}
    c["identb"] = np.eye(128, dtype=np.float32).astype(bf)
    c["identf"] = np.eye(128, dtype=np.float32)
    c["antij"] = np.eye(128, dtype=np.float32)[::-1].copy().astype(bf)
    c["onesf"] = np.ones((128, 128), np.float32)
    sel = np.zeros((128, 6, 12), np.float32)
    exp = np.zeros((12, 6, 128), np.float32)
    for i in range(6):
        sel[:64, i, 2 * i] = 1
        sel[64:, i, 2 * i + 1] = 1
        exp[2 * i, i, :64] = 1
        exp[2 * i + 1, i, 64:] = 1
    c["qksel"] = sel.astype(bf)
    c["qkexp"] = exp.astype(bf)
    E = np.zeros((33, 640), np.float32)
    idx = np.arange(640)
    d = idx - 255
    bk = _t5_bucket(d)
    for i in range(640):
        if d[i] >= 0:
            E[bk[i], i] += 1.0
            E[31, i] -= 1.0
        else:
            E[32, i] = NEGM
    c["erow"] = E
    pen = np.zeros((128, 16, 16), np.float32)
    for blk in range(16):
        pen[:, blk, blk:] = -1e30
    c["pen"] = pen
    oh = np.zeros((16, T), np.float32)
    for n in range(T // 256):
        oh[n, n * 256:(n + 1) * 256] = 1
    c["blkoh"] = oh.astype(bf)
    c["ones3"] = np.ones((3, T), np.float32).astype(bf)
    p = np.arange(128)[:, None]
    x = np.arange(896)[None, :]
    c["gm"] = np.where(x + p - 511 >= 0, 0.0, NEGM).astype(np.float32).astype(bf)
    c["tri"] = (np.arange(128)[:, None] <= np.arange(128)[None, :]).astype(np.float32)
    return c


CONST_SPECS = [("identb", (128, 128), BF16), ("identf", (128, 128), F32), ("antij", (128, 128), BF16),
               ("onesf", (128, 128), F32), ("qksel", (128, 6, 12), BF16), ("qkexp", (12, 6, 128), BF16),
               ("erow", (33, 640), F32), ("pen", (128, 16, 16), F32), ("blkoh", None, BF16),
               ("ones3", None, BF16), ("gm", (128, 896), BF16), ("tri", (128, 128), F32)]

PARAM_SPECS = [("w_in", (2, D, IN_DIM)), ("b_forget", (2, 6)), ("conv_w", (2, CONVW, 256)), ("conv_b", (2, 256)),
               ("conv_ln_g", (2, 256)), ("conv_ln_b", (2, 256)), ("moba_qn_g", (2, 64)), ("moba_kn_g", (2, 64)),
               ("fox_qn_g", (2, 64)), ("fox_kn_g", (2, 64)), ("rel_bias", (32, 6)), ("w_out", (2, D, D)),
               ("norm1_g", (2, D)), ("norm2_g", (2, D)), ("w_gate_up", (2, D, 2 * DFF)), ("w_down", (2, DFF, D))]


def build_program(T=4096, depth=2, phases="ABCD", dbg=()):
    assert T % 512 == 0
    NT = T // 512
    NJ = T // 128
    NB = T // 256
    nc = bass.Bass("TRN2", target_bir_lowering=False)
    S = Sched(nc)

    def din(name, shape, dt=F32):
        return nc.dram_tensor(name, list(shape), dt, kind="ExternalInput").ap()

    x_in = din("x", (T, D))
    P = {n: din(n, s) for n, s in PARAM_SPECS}
    C = {}
    for n, s, dt in CONST_SPECS:
        if n == "blkoh":
            s = (16, T)
        if n == "ones3":
            s = (3, T)
        C[n] = din("c_" + n, s, dt)
    out = nc.dram_tensor("out", [T, D], F32, kind="ExternalOutput").ap()
    dbg_out = {}
    for name, shape, dt in dbg:
        dbg_out[name] = nc.dram_tensor("dbg_" + name, list(shape), dt, kind="ExternalOutput").ap()

    def dscr(name, shape, dt):
        return nc.dram_tensor(name, list(shape), dt).ap()

    xa = dscr("xa", (T, D), F32)
    xb = dscr("xb", (T, D), F32)
    qm = dscr("qm", (NH, 64, T), BF16)
    nm = dscr("nm", (NH, 16, T), BF16)
    km = dscr("km", (NH, 64, T), BF16)
    qf = dscr("qf", (NH, 64, T), BF16)
    kf = dscr("kf", (NH, 64, T), BF16)
    cf = dscr("cf", (NH * 3 * NJ, 128), BF16)
    vs = dscr("vs", (T, 768), BF16)
    rrow = dscr("rrow", (NH, 640), F32)

    with ExitStack() as top:
        _nm = [0]

        def sb(es, name, shape, dt):
            _nm[0] += 1
            return es.enter_context(nc.sbuf_tensor("%s_%d" % (name, _nm[0]), list(shape), dt))

        psb = [top.enter_context(nc.psum_tensor("ps%d" % i, [128, 512], F32)) for i in range(8)]
        psbuf = S.bufs(8, "ps")
        pools = {"all": list(range(8))}
        prr = {}

        def set_pools(**kw):
            pools.clear()
            pools.update(kw)
            prr.clear()

        def psum(pool="all"):
            lst = pools[pool]
            i = prr.get(pool, 0)
            prr[pool] = (i + 1) % len(lst)
            k = lst[i]
            return psb[k], psbuf[k]

        identb = sb(top, "identb", (128, 128), BF16)
        identf = sb(top, "identf", (128, 128), F32)
        antij = sb(top, "antij", (128, 128), BF16)
        onesf = sb(top, "onesf", (128, 128), F32)
        qksel = sb(top, "qksel", (128, 6, 12), BF16)
        qkexp = sb(top, "qkexp", (12, 6, 128), BF16)
        pen = sb(top, "pen", (128, 16, 16), F32)
        gm = sb(top, "gm", (128, 896), BF16)
        tri = sb(top, "tri", (128, 128), F32)
        prmT = sb(top, "prmT", (128, 176), F32)
        bfb = sb(top, "bfb", (128, 2, 6), F32)
        gt = sb(top, "gt", (128, NH, 512), BF16)
        ones_row = sb(top, "ones_row", (128, 32), F32)
        b_const = S.buf("const")
        b_gt = S.buf("gt")

        ci = 0
        for nm_, t_ in (("identb", identb), ("identf", identf), ("antij", antij), ("onesf", onesf),
                        ("qksel", qksel), ("qkexp", qkexp), ("pen", pen), ("gm", gm), ("tri", tri)):
            S.dma("sp", "c%d" % ci, t_[:], C[nm_], writes=[b_const])
            ci += 1
        S.dma("sp", "c%d" % ci, bfb[:], P["b_forget"].rearrange("l h -> (l h)").partition_broadcast(128)
              .rearrange("p (l h) -> p l h", l=2), writes=[b_const])
        ci += 1
        S.op("dve", lambda e: e.memset(ones_row[:], 1.0), writes=[b_const])

        with ExitStack() as es0:
            prmA = sb(es0, "prmA", (128, 128), F32)
            prmB = sb(es0, "prmB", (48, 128), F32)
            erow = sb(es0, "erow", (33, 640), F32)
            tab = sb(es0, "tab", (33, NH), F32)
            rsb = sb(es0, "rsb", (NH, 640), F32)
            gst = sb(es0, "gst", (128, 512), F32)
            b_prm, b_tab, b_rsb, b_gst, b_rrow = S.bufs(5, "st")
            S.dma("sp", "p0", prmA[0:16, :], P["norm1_g"].rearrange("l (c p) -> (l c) p", p=128), writes=[b_prm])
            S.dma("sp", "p1", prmA[16:32, :], P["norm2_g"].rearrange("l (c p) -> (l c) p", p=128), writes=[b_prm])
            S.dma("sp", "p2", prmA[32:36, :], P["conv_b"].rearrange("l (c p) -> (l c) p", p=128), writes=[b_prm])
            S.dma("sp", "p3", prmA[36:40, :], P["conv_ln_g"].rearrange("l (c p) -> (l c) p", p=128), writes=[b_prm])
            S.dma("sp", "p4", prmA[40:44, :], P["conv_ln_b"].rearrange("l (c p) -> (l c) p", p=128), writes=[b_prm])
            k = 5
            for l in range(2):
                for gi, gname in enumerate(("moba_qn_g", "moba_kn_g", "fox_qn_g", "fox_kn_g")):
                    r = 44 + l * 4 + gi
                    for hf in range(2):
                        S.dma("sp", "p%d" % k, prmA[r:r + 1, hf * 64:(hf + 1) * 64], P[gname][l:l + 1, :], writes=[b_prm])
                        k += 1
            cwv = P["conv_w"].rearrange("l k (c p) -> (l k c) p", p=128)
            S.dma("sp", "p%d" % k, prmA[52:128, :], cwv[0:76, :], writes=[b_prm])
            k += 1
            S.dma("sp", "p%d" % k, prmB[:, :], cwv[76:124, :], writes=[b_prm])
            k += 1
            S.dma("sp", "p%d" % k, erow[:], C["erow"], writes=[b_prm])
            k += 1
            S.op("dve", lambda e: e.memset(tab[32:33, :], 1.0), writes=[b_tab])
            S.dma("sp", "p%d" % k, tab[0:32, :], P["rel_bias"], reads=[b_tab], writes=[b_tab])
            k += 1
            pt, bpt = psum()
            S.op("pe", lambda e: e.transpose(pt[:, 0:128], prmA[:, :], identf[:, :]),
                 reads=[b_prm, b_const], writes=[bpt])
            S.op("act", lambda e: e.activation(prmT[:, 0:128], pt[:, 0:128], AF.Copy), reads=[bpt], writes=[b_const])
            pt2, bpt2 = psum()
            S.op("pe", lambda e: e.transpose(pt2[:, 0:48], prmB[:, :], identf[0:48, 0:48]),
                 reads=[b_prm, b_const], writes=[bpt2])
            S.op("act", lambda e: e.activation(prmT[:, 128:176], pt2[:, 0:48], AF.Copy), reads=[bpt2], writes=[b_const])
            pr0, bpr0 = psum()
            pr1, bpr1 = psum()
            S.op("pe", lambda e: e.matmul(pr0[0:NH, 0:320], tab[:, :], erow[:, 0:320], start=True, stop=True),
                 reads=[b_tab, b_prm], writes=[bpr0])
            S.op("pe", lambda e: e.matmul(pr1[0:NH, 0:320], tab[:, :], erow[:, 320:640], start=True, stop=True),
                 reads=[b_tab, b_prm], writes=[bpr1])
            S.op("act", lambda e: e.activation(rsb[:, 0:320], pr0[0:NH, 0:320], AF.Copy), reads=[bpr0], writes=[b_rsb])
            S.op("act", lambda e: e.activation(rsb[:, 320:640], pr1[0:NH, 0:320], AF.Copy), reads=[bpr1], writes=[b_rsb])
            S.dma("sp", "p%d" % k, rrow, rsb[:], reads=[b_rsb], writes=[b_rrow])
            k += 1
            for h in range(NH):
                skew = AP(rrow.tensor, h * 640, [[1, 128], [1, 512]])
                S.dma("sp", "gsk", gst[:], skew, reads=[b_rrow], writes=[b_gst])
                S.op("act", lambda e, h=h: e.activation(gt[:, h, :], gst[:], AF.Copy), reads=[b_gst], writes=[b_gt])
            S.barrier()

        if "gt" in dbg_out:
            S.dma("sp", "dbg", dbg_out["gt"], gt[:].rearrange("p h x -> p (h x)"), reads=[b_gt])
            S.dma("sp", "dbg", dbg_out["prmT"], prmT[:], reads=[b_const])

        def rsqrt_newton(v, y, ti, t1, bv, by, bti, bt1, iters=2):
            S.op("dve", lambda e: e.tensor_scalar(ti, v.bitcast(I32), 9, None, ALU.arith_shift_right),
                 reads=[bv], writes=[bti])
            S.op("dve", lambda e: e.tensor_scalar(ti, ti, -1.0, float(0x5f3759df >> 8), ALU.mult, ALU.add),
                 reads=[bti], writes=[bti])
            S.op("dve", lambda e: e.tensor_scalar(y.bitcast(I32), ti, 8, None, ALU.logical_shift_left),
                 reads=[bti], writes=[by])
            for _ in range(iters):
                S.op("dve", lambda e: e.tensor_tensor(t1, y, y, ALU.mult), reads=[by], writes=[bt1])
                S.op("dve", lambda e: e.tensor_tensor(t1, t1, v, ALU.mult), reads=[bt1, bv], writes=[bt1])
                S.op("dve", lambda e: e.tensor_scalar(t1, t1, -0.5, 1.5, ALU.mult, ALU.add), reads=[bt1], writes=[bt1])
                S.op("dve", lambda e: e.tensor_tensor(y, y, t1, ALU.mult), reads=[by, bt1], writes=[by])

        def norm_part1(xs_aps, b_xs_list, nsub, tmp):
            ssq, rstd, rti, rt1, junk, hb = tmp["ssq"], tmp["rstd"], tmp["rti"], tmp["rt1"], tmp["junk"], tmp["hb"]
            bs = tmp["b"]
            for j in range(nsub):
                S.op("act", lambda e, j=j: e.activation(junk[:], xs_aps[j], AF.Square, accum_out=ssq[:, j:j + 1]),
                     reads=[b_xs_list[j]], writes=[bs["junk"], bs["ssq"]])
            S.op("dve", lambda e: e.tensor_scalar(ssq[:, 0:nsub], ssq[:, 0:nsub], 1.0 / D, RMS_EPS, ALU.mult, ALU.add),
                 reads=[bs["ssq"]], writes=[bs["ssq"]])
            rsqrt_newton(ssq[:, 0:nsub], rstd[:, 0:nsub], rti[:, 0:nsub], rt1[:, 0:nsub],
                         bs["ssq"], bs["rstd"], bs["rti"], bs["rt1"], iters=2)
            for j in range(nsub):
                S.op("dve", lambda e, j=j: e.tensor_scalar(hb[j][:], xs_aps[j], rstd[:, j:j + 1], None, ALU.mult),
                     reads=[b_xs_list[j], bs["rstd"]], writes=[bs["hb"][j]])

        def norm_part2(nsub, hT_t, b_hT, tmp):
            hb = tmp["hb"]
            bs = tmp["b"]
            for j in range(nsub):
                pt_, bpt_ = psum("tp")
                ptb = pt_[:].bitcast(BF16)

                def tr(e, j=j, ptb=ptb):
                    ins = None
                    for c in range(NCH):
                        ins = e.transpose(ptb[:, c * 128:(c + 1) * 128], hb[j][:, c * 128:(c + 1) * 128], identb[:, :])
                    return ins
                S.op("pe", tr, reads=[bs["hb"][j], b_const], writes=[bpt_])
                S.op("act", lambda e, j=j, ptb=ptb: e.activation(
                    hT_t[:, :, j * 128:(j + 1) * 128], ptb.rearrange("p (c t) -> p c t", c=NCH), AF.Copy),
                    reads=[bpt_], writes=[b_hT])

        def make_norm_tmp(es, nsub):
            return dict(ssq=sb(es, "ssq", (128, 4), F32), rstd=sb(es, "rstd", (128, 4), F32),
                        rti=sb(es, "rti", (128, 4), I32), rt1=sb(es, "rt1", (128, 4), F32),
                        junk=sb(es, "junk", (128, D), BF16),
                        hb=[sb(es, "hb%d" % i, (128, D), BF16) for i in range(nsub)],
                        b=dict(ssq=S.buf(), rstd=S.buf(), rti=S.buf(), rt1=S.buf(), junk=S.buf(), hb=S.bufs(nsub)))

        mixA = sb(top, "mixA", (128, 2, T), BF16)
        b_mixA = S.buf("mixA")
        negc = sb(top, "negc", (128, NJ, NH), F32)
        b_negc = S.buf("negc")

        for l in range(depth):
            x_src = x_in if l == 0 else xb
            x_dst = xb if l == 0 else out
            if l == depth - 1:
                x_dst = out
            if "A" in phases:
                with ExitStack() as es:
                    set_pools(tp=[0, 1], pj=[2, 3, 4, 7], ss=[5, 6])
                    wb = sb(es, "wb", (128, NCH, IN_DIM), BF16)
                    pc = 0 + l * NCH
                    wgrp = [(C_QB, C_VB), (C_AV, C_QB), (C_QC, C_VC), (C_VB, C_QC), (C_VC, IN_DIM)]
                    b_wg = S.bufs(len(wgrp), "wbg")
                    w_in_v = P["w_in"][l].rearrange("(c p) f -> p c f", p=128)
                    for gi_, (c0_, c1_) in enumerate(wgrp):
                        S.dma("pool", "wb%d" % gi_, wb[:, :, c0_:c1_], w_in_v[:, :, c0_:c1_],
                              writes=[b_wg[gi_]], max_dma_last_dim=4096)

                    def wbuf(col0):
                        for gi_, (c0_, c1_) in enumerate(wgrp):
                            if c0_ <= col0 < c1_:
                                return b_wg[gi_]
                        raise ValueError(col0)
                    xq = [sb(es, "xq%d" % i, (128, D), F32) for i in range(4)]
                    b_xq = S.bufs(4, "xq")
                    hT = [sb(es, "hT%d" % i, (128, NCH, 512), BF16) for i in range(2)]
                    b_hT = S.bufs(2, "hT")
                    tmp = make_norm_tmp(es, 4)
                    hglu = sb(es, "hglu", (128, 2, 2, 544), BF16)
                    b_hglu = S.bufs(2, "hglu")
                    diag = sb(es, "diag", (128, 2, CONVW, 128), BF16)
                    b_diag = S.buf("diag")
                    cwh = sb(es, "cwh", (128, 62), F32)
                    gq8 = sb(es, "gq8", (128, 4), F32)
                    b_sm = S.buf("small")
                    kmean = sb(es, "kmean", (128, 3, 16), F32)
                    kmeanb = sb(es, "kmeanb", (128, 3, 16), BF16)
                    b_kmean = S.buf("kmean")
                    fl = sb(es, "fl", (128, NJ, NH), F32)
                    b_fl = S.buf("fl")
                    qraw = [[[sb(es, "qraw%d_%d_%d" % (p_, g, i), (128, 512), BF16) for i in range(6)] for g in range(2)] for p_ in range(2)]
                    b_qraw = [[S.bufs(6, "qraw%d_%d_" % (p_, g)) for g in range(2)] for p_ in range(2)]
                    qsv = [[sb(es, "qsv%d_%d" % (p_, g), (128, 48), F32) for g in range(2)] for p_ in range(2)]
                    b_qsv = [S.bufs(2, "qsv%d_" % p_) for p_ in range(2)]
                    sqb = [sb(es, "sq%d" % i, (128, 512), BF16) for i in range(3)]
                    b_sq = S.bufs(3, "sq")
                    qn = [sb(es, "qn%d" % i, (128, 512), BF16) for i in range(6)]
                    b_qn = S.bufs(6, "qn")
                    qs = [[{k_: sb(es, "qs%d_%d_%s" % (p_, g, k_), (128, 48), dt_) for k_, dt_ in
                            (("y", F32), ("ti", I32), ("t1", F32), ("hi", BF16), ("lo", BF16))} for g in range(2)] for p_ in range(2)]
                    b_qs = [[{k_: S.buf("qs%d_%d_%s" % (p_, g, k_)) for k_ in qs[p_][g]} for g in range(2)] for p_ in range(2)]
                    rsT = [sb(es, "rsT%d" % g, (12, 1024), BF16) for g in range(2)]
                    b_rsT = S.bufs(2, "rsT")
                    th = [sb(es, "th%d" % i, (128, 512), F32) for i in range(2)]
                    b_th = S.bufs(2, "th")
                    ycv = [sb(es, "ycv%d" % i, (128, 512), F32) for i in range(2)]
                    b_ycv = S.bufs(2, "ycv")
                    ysq = th
                    b_ysq = b_th
                    lns = {k_: sb(es, "lns_" + k_, (128, 8), dt_) for k_, dt_ in
                           (("s", F32), ("v", F32), ("y", F32), ("ti", I32), ("t1", F32), ("ab", F32), ("abf", F32), ("hi", BF16), ("lo", BF16))}
                    b_lns = {k_: S.buf("lns_" + k_) for k_ in lns}
                    abT = sb(es, "abT", (2, 1024), BF16)
                    b_abT = S.buf("abT")
                    selab = sb(es, "selab", (2, 2, 128), BF16)
                    onescol = sb(es, "onescol", (128, 1), F32)
                    vst = [sb(es, "vst%d" % i, (128, 768), BF16) for i in range(2)]
                    b_vst = S.bufs(2, "vst")
                    gsb = sb(es, "gsb", (128, 4, NH, 16), F32)
                    m8 = sb(es, "m8", (128, 4, NH, 8), F32)
                    nmk = sb(es, "nmk", (128, 4, NH, 16), F32)
                    nmb = sb(es, "nmb", (128, 4, NH * 16), BF16)
                    nmT = sb(es, "nmT", (96, 512), BF16)
                    b_gsb, b_m8, b_nmk, b_nmb, b_nmT = S.bufs(5, "gate")

                    S.op("dve", lambda e: e.tensor_scalar(cwh[:], prmT[:, 52 + l * 62:52 + (l + 1) * 62], 0.5, None, ALU.mult),
                         reads=[b_const], writes=[b_sm])
                    gb = 44 + l * 4
                    S.op("dve", lambda e: e.tensor_scalar(gq8[:, 0:1], prmT[:, gb:gb + 1], 0.125, None, ALU.mult), reads=[b_const], writes=[b_sm])
                    S.op("dve", lambda e: e.tensor_copy(gq8[:, 1:2], prmT[:, gb + 1:gb + 2]), reads=[b_const], writes=[b_sm])
                    S.op("dve", lambda e: e.tensor_scalar(gq8[:, 2:3], prmT[:, gb + 2:gb + 3], 0.125, None, ALU.mult), reads=[b_const], writes=[b_sm])
                    S.op("dve", lambda e: e.tensor_copy(gq8[:, 3:4], prmT[:, gb + 3:gb + 4]), reads=[b_const], writes=[b_sm])
                    S.op("dve", lambda e: e.memset(onescol[:], 1.0), writes=[b_sm])
                    S.op("dve", lambda e: e.memset(selab[:], 0.0), writes=[b_sm])
                    S.op("dve", lambda e: e.memset(selab[0:1, 0, :], 1.0), writes=[b_sm])
                    S.dma("sp", "selab", selab[1:2, 1, :], selab[0:1, 0, :], reads=[b_sm], writes=[b_sm])
                    for k_ in range(CONVW):
                        for ch in range(2):
                            S.op("act", lambda e, k_=k_, ch=ch: e.activation(
                                diag[:, ch, k_, :], identf[:, :], AF.Copy, scale=cwh[:, k_ * 2 + ch:k_ * 2 + ch + 1]),
                                reads=[b_const, b_sm], writes=[b_diag])
                    S.op("pool", lambda e: e.memset(hglu[:, 0, :, 0:32], 0.0), writes=[b_hglu[0]])
                    S.op("pool", lambda e: e.memset(kmean[:], 0.0), writes=[b_kmean])
                    S.op("pool", lambda e: e.memset(kmeanb[:], 0.0), writes=[b_kmean])

                    def load_x(tt):
                        for j in range(4):
                            r0 = tt * 512 + j * 128
                            S.dma("sp", "xq%d" % j, xq[j][:], x_src[r0:r0 + 128, :], writes=[b_xq[j]])

                    def s1_part1(tt):
                        norm_part1([xq[j][:] for j in range(4)], b_xq, 4, tmp)
                        if tt + 1 < NT:
                            load_x(tt + 1)

                    def s1_part2(tt):
                        norm_part2(4, hT[tt % 2], b_hT[tt % 2], tmp)

                    def proj_fm(tt, col0):
                        hTc, bhTc = hT[tt % 2], b_hT[tt % 2]
                        ps_, bps_ = psum("pj")

                        def f(e):
                            ins = None
                            for c in range(NCH):
                                ins = e.matmul(ps_[:, :], wb[:, c, col0:col0 + 128], hTc[:, c, :],
                                               start=(c == 0), stop=(c == NCH - 1))
                            return ins
                        S.op("pe", f, reads=[bhTc, wbuf(col0)], writes=[bps_])
                        return ps_, bps_

                    sqi = [0]
                    pssT = [None, None]

                    def qk_proj(tt, grp):
                        qcol = C_QB if grp == 0 else C_QC
                        kcol = C_KB if grp == 0 else C_KC
                        pss, bpss = psum("ss")
                        pend = []

                        def emit_fss(i, sq_, bsq_):
                            def fss(e):
                                ins = None
                                for j in range(4):
                                    ins = e.matmul(pss[:, j * 12:(j + 1) * 12], sq_[:, j * 128:(j + 1) * 128], qksel[:, i, :],
                                                   start=(i == 0 and j == 0), stop=(i == 5 and j == 3), skip_group_check=True)
                                return ins
                            S.op("pe", fss, reads=[bsq_, b_const], writes=[bpss])
                        for i in range(6):
                            col0 = (qcol + i * 128) if i < 3 else (kcol + (i - 3) * 128)
                            pq_, bpq_ = proj_fm(tt, col0)
                            sq_ = sqb[sqi[0] % 3]
                            bsq_ = b_sq[sqi[0] % 3]
                            sqi[0] += 1
                            S.op("act", lambda e: e.activation(sq_[:], pq_[:, :], AF.Square), reads=[bpq_], writes=[bsq_])
                            S.op("act", lambda e: e.activation(qraw[tt % 2][grp][i][:], pq_[:, :], AF.Copy), reads=[bpq_],
                                 writes=[b_qraw[tt % 2][grp][i]])
                            if pend:
                                emit_fss(*pend.pop(0))
                            pend.append((i, sq_, bsq_))
                        while pend:
                            emit_fss(*pend.pop(0))
                        S.op("dve", lambda e: e.tensor_scalar(qsv[tt % 2][grp][:], pss[:, 0:48], 1.0 / 64, RMS_EPS, ALU.mult, ALU.add),
                             reads=[bpss], writes=[b_qsv[tt % 2][grp]])
                        q_, bq_ = qs[tt % 2][grp], b_qs[tt % 2][grp]
                        v_, bv_ = qsv[tt % 2][grp], b_qsv[tt % 2][grp]
                        rsqrt_newton(v_[:], q_["y"][:], q_["ti"][:], q_["t1"][:], bv_, bq_["y"], bq_["ti"], bq_["t1"], iters=2)
                        S.op("dve", lambda e: e.tensor_copy(q_["hi"][:], q_["y"][:]), reads=[bq_["y"]], writes=[bq_["hi"]])
                        S.op("dve", lambda e: e.tensor_copy(q_["t1"][:], q_["hi"][:]), reads=[bq_["hi"]], writes=[bq_["t1"]])
                        S.op("dve", lambda e: e.tensor_tensor(q_["lo"][:], q_["y"][:], q_["t1"][:], ALU.subtract),
                             reads=[bq_["y"], bq_["t1"]], writes=[bq_["lo"]])

                    def qk_chain1(tt, grp):
                        q_, bq_ = qs[tt % 2][grp], b_qs[tt % 2][grp]
                        ptr, bptr = psum("tp")
                        ptrb = ptr[:].bitcast(BF16)

                        def ftr(e):
                            ins = None
                            for w_, key in enumerate(("hi", "lo")):
                                for j in range(4):
                                    ins = e.transpose(ptrb[0:12, w_ * 512 + j * 128:w_ * 512 + (j + 1) * 128],
                                                      q_[key][:, j * 12:(j + 1) * 12], identb[:, :])
                            return ins
                        S.op("pe", ftr, reads=[bq_["hi"], bq_["lo"], b_const], writes=[bptr])
                        S.op("act", lambda e: e.activation(rsT[grp][:, :], ptrb[0:12, :], AF.Copy), reads=[bptr], writes=[b_rsT[grp]])

                    def qk_chain2(tt, grp):
                        t0 = tt * 512
                        for i in range(6):
                            pbc, bpbc = psum("pj")

                            def fbc(e):
                                e.matmul(pbc[:, :], qkexp[:, i, :], rsT[grp][:, 0:512], start=True, stop=False)
                                return e.matmul(pbc[:, :], qkexp[:, i, :], rsT[grp][:, 512:1024], start=False, stop=True)
                            S.op("pe", fbc, reads=[b_rsT[grp], b_const], writes=[bpbc])
                            gcol = (0 if i < 3 else 1) + 2 * grp
                            S.op("dve", lambda e: e.scalar_tensor_tensor(
                                qn[i][:], pbc[:, :], gq8[:, gcol:gcol + 1], qraw[tt % 2][grp][i][:], ALU.mult, ALU.mult),
                                reads=[bpbc, b_sm, b_qraw[tt % 2][grp][i]], writes=[b_qn[i]])
                            if grp == 0 and i >= 3:
                                S.op("dve", lambda e: e.tensor_reduce(
                                    kmean[:, i - 3, 2 * tt:2 * tt + 2], qn[i][:].rearrange("p (b s) -> p b s", b=2),
                                    AX.X, ALU.add), reads=[b_qn[i]], writes=[b_kmean])
                            if grp == 0 and i == 2:
                                gate_q = True
                            dst = (qm, km, qf, kf)[(0 if i < 3 else 1) + 2 * grp]
                            hp = (i % 3) * 2
                            S.dma("sp", "qn%d" % i, dst[hp:hp + 2, :, t0:t0 + 512].rearrange("h d t -> (h d) t"),
                                  qn[i][:], reads=[b_qn[i]])

                    def gate(tt):
                        t0 = tt * 512
                        S.op("dve", lambda e: e.tensor_scalar(kmeanb[:, :, 2 * tt:2 * tt + 2], kmean[:, :, 2 * tt:2 * tt + 2],
                                                               1.0 / 256, None, ALU.mult),
                             reads=[b_kmean], writes=[b_kmean])
                        if tt > 0:
                            pg2 = [psum("pj"), psum("pj")]

                            def fg(e):
                                ins = None
                                for hf in range(2):
                                    for j in range(4):
                                        for i_ in range(3):
                                            o_ = (j * 3 + i_) * 16
                                            ins = e.matmul(pg2[hf][0][:, o_:o_ + 16],
                                                           qn[i_][hf * 64:(hf + 1) * 64, j * 128:(j + 1) * 128],
                                                           kmeanb[hf * 64:(hf + 1) * 64, i_, :], start=True, stop=True)
                                return ins
                            S.op("pe", fg, reads=[b_qn[0], b_qn[1], b_qn[2], b_kmean], writes=[pg2[0][1], pg2[1][1]])
                            for j in range(4):
                                blk = 2 * tt + j // 2
                                for hf in range(2):
                                    S.op("dve", lambda e, j=j, hf=hf, blk=blk: e.tensor_tensor(
                                        gsb[:, j, hf * 3:(hf + 1) * 3, :],
                                        pg2[hf][0][:, j * 48:(j + 1) * 48].rearrange("p (h n) -> p h n", h=3),
                                        pen[:, blk, :].unsqueeze(1).to_broadcast([128, 3, 16]), ALU.add),
                                        reads=[pg2[hf][1], b_const], writes=[b_gsb])
                            for j in range(4):
                                for h in range(NH):
                                    S.op("dve", lambda e, j=j, h=h: e.max(m8[:, j, h, :], gsb[:, j, h, :]),
                                         reads=[b_gsb], writes=[b_m8])
                            S.op("dve", lambda e: e.tensor_tensor(
                                nmk[:].rearrange("p j h n -> p (j h) n"), gsb[:].rearrange("p j h n -> p (j h) n"),
                                m8[:].rearrange("p j h n -> p (j h) n")[:, :, 2:3].to_broadcast([128, 4 * NH, 16]),
                                ALU.is_lt), reads=[b_gsb, b_m8], writes=[b_nmk])
                            nmb5 = nmb[:].rearrange("p j (i f n) -> p j i f n", i=3, f=2)
                            for hf in range(2):
                                S.op("dve", lambda e, hf=hf: e.tensor_scalar(
                                    nmb5[:, :, :, hf, :], nmk[:, :, hf * 3:(hf + 1) * 3, :],
                                    NEGM, None, ALU.mult), reads=[b_nmk], writes=[b_nmb])
                        else:
                            S.op("dve", lambda e: e.memset(nmb[:], 0.0), writes=[b_nmb])

                    def gate2(tt):
                        t0 = tt * 512
                        ptn, bptn = psum("tp")
                        ptnb = ptn[:].bitcast(BF16)

                        def ftn(e):
                            ins = None
                            for j in range(4):
                                ins = e.transpose(ptnb[0:96, j * 128:(j + 1) * 128], nmb[:, j, :], identb[:, :])
                            return ins
                        S.op("pe", ftn, reads=[b_nmb, b_const], writes=[bptn])
                        S.op("act", lambda e: e.activation(nmT[:, :], ptnb[0:96, 0:512], AF.Copy), reads=[bptn], writes=[b_nmT])
                        S.dma("sp", "nmT", nm[:, :, t0:t0 + 512].rearrange("h n t -> (h n) t"), nmT[:, :], reads=[b_nmT])

                    def conv_proj(tt):
                        cur = tt % 2
                        for ch in range(2):
                            pv_, bpv_ = proj_fm(tt, C_AV + ch * 128)
                            pg_, bpg_ = proj_fm(tt, C_AG + ch * 128)
                            S.op("act", lambda e, ch=ch, pg_=pg_: e.activation(th[ch][:], pg_[:, :], AF.Tanh, scale=0.5),
                                 reads=[bpg_], writes=[b_th[ch]])
                            S.op("dve", lambda e, ch=ch, pv_=pv_: e.scalar_tensor_tensor(
                                hglu[:, cur, ch, 32:544], th[ch][:], 1.0, pv_[:, :], ALU.add, ALU.mult),
                                reads=[b_th[ch], bpv_], writes=[b_hglu[cur]])

                    def pad_copy(tt):
                        cur = tt % 2
                        if tt + 1 < NT:
                            S.op("pool", lambda e: e.tensor_copy(hglu[:, 1 - cur, :, 0:32], hglu[:, cur, :, 512:544]),
                                 reads=[b_hglu[cur]], writes=[b_hglu[1 - cur]])

                    def v_proj(tt):
                        hTc, bhTc = hT[tt % 2], b_hT[tt % 2]
                        t0 = tt * 512
                        for j in range(4):
                            pv1, bpv1 = psum("pj")
                            pv2, bpv2 = psum("pj")

                            def fv(e, j=j, pv1=pv1, pv2=pv2):
                                ins = None
                                for c in range(NCH):
                                    e.matmul(pv1[:, 0:384], hTc[:, c, j * 128:(j + 1) * 128], wb[:, c, C_VB:C_VB + 384],
                                             start=(c == 0), stop=(c == NCH - 1))
                                for c in range(NCH):
                                    ins = e.matmul(pv2[:, 0:390], hTc[:, c, j * 128:(j + 1) * 128], wb[:, c, C_VC:C_VC + 390],
                                                   start=(c == 0), stop=(c == NCH - 1))
                                return ins
                            S.op("pe", fv, reads=[bhTc, wbuf(C_VB), wbuf(C_VC)], writes=[bpv1, bpv2])
                            vj = vst[j % 2]
                            bvj = b_vst[j % 2]
                            S.op("act", lambda e, vj=vj, pv1=pv1: e.activation(vj[:, 0:384], pv1[:, 0:384], AF.Copy),
                                 reads=[bpv1], writes=[bvj])
                            S.op("act", lambda e, vj=vj, pv2=pv2: e.activation(vj[:, 384:768], pv2[:, 0:384], AF.Copy),
                                 reads=[bpv2], writes=[bvj])
                            jj = tt * 4 + j
                            S.op("act", lambda e, jj=jj, pv2=pv2: e.activation(fl[:, jj, :], pv2[:, 384:390], AF.Copy),
                                 reads=[bpv2], writes=[b_fl])
                            S.dma("sp", "vst%d" % (j % 2), vs[t0 + j * 128:t0 + (j + 1) * 128, :], vj[:, :], reads=[bvj])

                    lnst = {}

                    def conv_mm(tt):
                        cur = tt % 2
                        for ch in range(2):
                            pcv, bpcv = psum("pj")

                            def fcv(e, ch=ch, pcv=pcv):
                                ins = None
                                for k_ in range(CONVW):
                                    o0 = 2 + k_
                                    ins = e.matmul(pcv[:, :], diag[:, ch, k_, :], hglu[:, cur, ch, o0:o0 + 512],
                                                   start=(k_ == 0), stop=(k_ == CONVW - 1))
                                return ins
                            S.op("pe", fcv, reads=[b_diag, b_hglu[cur]], writes=[bpcv])
                            cbc = 32 + l * 2 + ch
                            S.op("act", lambda e, ch=ch, pcv=pcv, cbc=cbc: e.activation(
                                ycv[ch][:], pcv[:, :], AF.Identity, bias=prmT[:, cbc:cbc + 1]),
                                reads=[bpcv, b_const], writes=[b_ycv[ch]])
                            S.op("act", lambda e, ch=ch, pcv=pcv, cbc=cbc: e.activation(
                                ysq[ch][:], pcv[:, :], AF.Square, bias=prmT[:, cbc:cbc + 1]),
                                reads=[bpcv, b_const], writes=[b_ysq[ch]])
                        pst, bpst = psum("pj")

                        def fst(e):
                            ins = None
                            first = True
                            for j in range(4):
                                for w_, src in enumerate((ycv, ysq)):
                                    for ch in range(2):
                                        ins = e.matmul(pst[:, j * 2 + w_:j * 2 + w_ + 1], src[ch][:, j * 128:(j + 1) * 128], onescol[:, :],
                                                       start=first, stop=(j == 3 and w_ == 1 and ch == 1), skip_group_check=True)
                                        first = False
                            return ins
                        S.op("pe", fst, reads=b_ycv + b_ysq + [b_sm], writes=[bpst])
                        S.op("dve", lambda e: e.tensor_scalar(lns["s"][:], pst[:, 0:8], 1.0 / 256, None, ALU.mult),
                             reads=[bpst], writes=[b_lns["s"]])

                    def ln_chain(tt):
                        t0 = tt * 512
                        L_, bL = lns, b_lns
                        s3 = L_["s"][:].rearrange("p (j w) -> p j w", w=2)
                        v4 = L_["v"][:, 0:4]
                        S.op("dve", lambda e: e.tensor_tensor(L_["t1"][:, 0:4], s3[:, :, 0], s3[:, :, 0], ALU.mult), reads=[bL["s"]], writes=[bL["t1"]])
                        S.op("dve", lambda e: e.tensor_tensor(v4, s3[:, :, 1], L_["t1"][:, 0:4], ALU.subtract), reads=[bL["s"], bL["t1"]], writes=[bL["v"]])
                        S.op("dve", lambda e: e.tensor_scalar(v4, v4, LN_EPS, None, ALU.add), reads=[bL["v"]], writes=[bL["v"]])
                        rsqrt_newton(v4, L_["y"][:, 0:4], L_["ti"][:, 0:4], L_["t1"][:, 0:4], bL["v"], bL["y"], bL["ti"], bL["t1"], iters=2)
                        ab3 = L_["ab"][:].rearrange("p (j w) -> p j w", w=2)
                        S.op("dve", lambda e: e.tensor_copy(ab3[:, :, 0], L_["y"][:, 0:4]), reads=[bL["y"]], writes=[bL["ab"]])
                        S.op("dve", lambda e: e.tensor_tensor(ab3[:, :, 1], s3[:, :, 0], L_["y"][:, 0:4], ALU.mult), reads=[bL["s"], bL["y"]], writes=[bL["ab"]])
                        S.op("dve", lambda e: e.tensor_copy(L_["hi"][:], L_["ab"][:]), reads=[bL["ab"]], writes=[bL["hi"]])
                        S.op("dve", lambda e: e.tensor_copy(L_["abf"][:], L_["hi"][:]), reads=[bL["hi"]], writes=[bL["abf"]])
                        S.op("dve", lambda e: e.tensor_tensor(L_["lo"][:], L_["ab"][:], L_["abf"][:], ALU.subtract), reads=[bL["ab"], bL["abf"]], writes=[bL["lo"]])

                    def ln_pe(tt):
                        t0 = tt * 512
                        L_, bL = lns, b_lns
                        ptr, bptr = psum("tp")
                        ptrb = ptr[:].bitcast(BF16)

                        def ftr(e):
                            ins = None
                            for w_, key in enumerate(("hi", "lo")):
                                for j in range(4):
                                    ins = e.transpose(ptrb[0:2, w_ * 512 + j * 128:w_ * 512 + (j + 1) * 128],
                                                      L_[key][:, j * 2:(j + 1) * 2], identb[:, :])
                            return ins
                        S.op("pe", ftr, reads=[bL["hi"], bL["lo"], b_const], writes=[bptr])
                        S.op("act", lambda e: e.activation(abT[:, :], ptrb[0:2, :], AF.Copy), reads=[bptr], writes=[b_abT])
                        pA, bpA = psum("pj")
                        pB, bpB = psum("pj")

                        def fab(e):
                            e.matmul(pA[:, :], selab[:, 0, :], abT[:, 0:512], start=True, stop=False)
                            e.matmul(pA[:, :], selab[:, 0, :], abT[:, 512:1024], start=False, stop=True)
                            e.matmul(pB[:, :], selab[:, 1, :], abT[:, 0:512], start=True, stop=False)
                            return e.matmul(pB[:, :], selab[:, 1, :], abT[:, 512:1024], start=False, stop=True)
                        S.op("pe", fab, reads=[b_abT, b_sm], writes=[bpA, bpB])
                        for ch in range(2):
                            S.op("dve", lambda e, ch=ch: e.tensor_tensor(ycv[ch][:], ycv[ch][:], pA[:, :], ALU.mult),
                                 reads=[b_ycv[ch], bpA], writes=[b_ycv[ch]])
                            S.op("dve", lambda e, ch=ch: e.tensor_tensor(ycv[ch][:], ycv[ch][:], pB[:, :], ALU.subtract),
                                 reads=[b_ycv[ch], bpB], writes=[b_ycv[ch]])
                            gc_ = 36 + l * 2 + ch
                            bc_ = 40 + l * 2 + ch
                            S.op("act", lambda e, ch=ch, gc_=gc_, bc_=bc_: e.activation(
                                mixA[:, ch, t0:t0 + 512], ycv[ch][:], AF.Silu,
                                scale=prmT[:, gc_:gc_ + 1], bias=prmT[:, bc_:bc_ + 1]),
                                reads=[b_ycv[ch], b_const], writes=[b_mixA])

                    load_x(0)
                    s1_part1(0)
                    s1_part2(0)
                    for gi_, (c0_, c1_) in enumerate(wgrp):
                        S.op("dve", lambda e, c0_=c0_, c1_=c1_: e.tensor_tensor(
                            wb[:, :, c0_:c1_], wb[:, :, c0_:c1_],
                            prmT[:, pc:pc + NCH].unsqueeze(2).to_broadcast([128, NCH, c1_ - c0_]), ALU.mult),
                            reads=[b_wg[gi_], b_const], writes=[b_wg[gi_]])
                    for tt in range(NT + 2):
                        prod = tt < NT
                        cons = 0 < tt <= NT
                        if prod:
                            qk_proj(tt, 0)
                        if tt + 1 < NT and tt < NT:
                            s1_part1(tt + 1)
                        if tt >= 2:
                            ln_pe(tt - 2)
                        if cons:
                            qk_chain1(tt - 1, 0)
                            qk_chain1(tt - 1, 1)
                        if prod:
                            qk_proj(tt, 1)
                        if cons:
                            qk_chain2(tt - 1, 0)
                            gate(tt - 1)
                        if prod:
                            conv_proj(tt)
                        if cons:
                            qk_chain2(tt - 1, 1)
                            conv_mm(tt - 1)
                        if prod:
                            pad_copy(tt)
                            v_proj(tt)
                        if cons:
                            gate2(tt - 1)
                            ln_chain(tt - 1)
                        if tt + 1 < NT:
                            s1_part2(tt + 1)

                    with ExitStack() as esf:
                        W = NJ * NH

                        def v3(t_):
                            ap_ = t_[:] if t_[:].dtype == F32 else t_[:].bitcast(F32)
                            return ap_[:, 0:W].rearrange("p (j h) -> p j h", h=NH)
                        z, ez, sp_, tot, inc, hif = [v3(qraw[0][0][i_]) for i_ in range(6)]
                        r1 = v3(ycv[0])
                        c3 = ycv[1][:].bitcast(BF16)[:, 0:NH * 3 * NJ].rearrange("p (h r j) -> p h r j", h=NH, r=3)
                        cT = sqb[0]
                        b_z, b_ez, b_sp, b_tot, b_inc, b_hif = b_qraw[0][0]
                        b_r1, b_c3, b_cT = b_ycv[0], b_ycv[1], b_sq[0]
                        S.op("dve", lambda e: e.tensor_tensor(z[:], fl[:], bfb[:, l, :].unsqueeze(1).to_broadcast([128, NJ, NH]), ALU.add),
                             reads=[b_fl, b_const], writes=[b_z])
                        S.op("act", lambda e: e.activation(ez[:], z[:], AF.Exp, scale=-1.0), reads=[b_z], writes=[b_ez])
                        S.op("act", lambda e: e.activation(sp_[:], ez[:], AF.Ln, bias=1.0), reads=[b_ez], writes=[b_sp])
                        pw, bpw = psum("pj")
                        pt_, bpt_ = psum("pj")
                        spf = sp_[:].rearrange("p j h -> p (j h)")
                        S.op("pe", lambda e: e.matmul(pw[:, 0:W], tri[:, :], spf, start=True, stop=True), reads=[b_sp, b_const], writes=[bpw])
                        S.op("pe", lambda e: e.matmul(pt_[:, 0:W], onesf[:, :], spf, start=True, stop=True), reads=[b_sp, b_const], writes=[bpt_])
                        S.op("act", lambda e: e.activation(tot[:].rearrange("p j h -> p (j h)"), pt_[:, 0:W], AF.Copy), reads=[bpt_], writes=[b_tot])
                        for h in range(NH):
                            S.op("dve", lambda e, h=h: e.tensor_tensor_scan(inc[:, :, h], ones_row[:, 0:NJ], tot[:, :, h], 0.0, ALU.mult, ALU.add),
                                 reads=[b_tot, b_const], writes=[b_inc])
                        S.op("dve", lambda e: e.tensor_tensor(inc[:], inc[:], tot[:], ALU.subtract), reads=[b_inc, b_tot], writes=[b_inc])
                        S.op("dve", lambda e: e.tensor_tensor(negc[:].rearrange("p j h -> p (j h)"), pw[:, 0:W], inc[:].rearrange("p j h -> p (j h)"), ALU.add),
                             reads=[bpw, b_inc], writes=[b_negc])
                        c3v = [c3[:, :, r_, :].rearrange("p h j -> p j h") for r_ in range(3)]
                        S.op("dve", lambda e: e.tensor_scalar(c3v[0], negc[:], -1.0, None, ALU.mult), reads=[b_negc], writes=[b_c3])
                        S.op("dve", lambda e: e.tensor_copy(hif[:], c3v[0]), reads=[b_c3], writes=[b_hif])
                        S.op("dve", lambda e: e.scalar_tensor_tensor(r1[:], negc[:], -1.0, hif[:], ALU.mult, ALU.subtract),
                             reads=[b_negc, b_hif], writes=[b_r1])
                        S.op("dve", lambda e: e.tensor_copy(c3v[1], r1[:]), reads=[b_r1], writes=[b_c3])
                        S.op("dve", lambda e: e.tensor_copy(hif[:], c3v[1]), reads=[b_c3], writes=[b_hif])
                        S.op("dve", lambda e: e.tensor_tensor(r1[:], r1[:], hif[:], ALU.subtract), reads=[b_r1, b_hif], writes=[b_r1])
                        S.op("dve", lambda e: e.tensor_copy(c3v[2], r1[:]), reads=[b_r1], writes=[b_c3])
                        X = NH * 3 * NJ
                        c3f = c3[:].rearrange("p h r j -> p (h r j)")
                        for g0 in range(0, X, 128):
                            n_ = min(128, X - g0)
                            ptc, bptc = psum("tp")
                            ptcb = ptc[:].bitcast(BF16)
                            S.op("pe", lambda e, g0=g0, n_=n_, ptcb=ptcb: e.transpose(ptcb[0:n_, 0:128], c3f[:, g0:g0 + n_], identb[:, :]),
                                 reads=[b_c3, b_const], writes=[bptc])
                            S.op("act", lambda e, n_=n_, ptcb=ptcb: e.activation(cT[0:n_, 0:128], ptcb[0:n_, 0:128], AF.Copy), reads=[bptc], writes=[b_cT])
                            S.dma("sp", "cT", cf[g0:g0 + n_, :], cT[0:n_, 0:128], reads=[b_cT])
                S.barrier()

            _nm[0] += 1
            wd_cm = nc.sbuf_tensor("wd_%d" % _nm[0], [128, NFC, D], BF16)
            wd = wd_cm.__enter__()
            b_wd = S.buf("wd")
            if "D" in phases:
                S.dma("pool", "wd", wd[:, :, :], P["w_down"][l].rearrange("(f p) d -> p f d", p=128), writes=[b_wd],
                      max_dma_last_dim=4096)
            _nm[0] += 1
            wo_cm = nc.sbuf_tensor("wo_%d" % _nm[0], [128, NCH, D], BF16)
            wo = wo_cm.__enter__()
            b_wo = S.bufs(NCH, "wo")
            if "C" in phases:
                for c in range(NCH):
                    S.dma("pool", "wb%d" % c, wo[:, c, :], P["w_out"][l, c * 128:(c + 1) * 128, :], writes=[b_wo[c]],
                          max_dma_last_dim=4096)
            with ExitStack() as es:
                mixB = sb(es, "mixB", (128, 6, T), BF16)
                b_mixB = S.buf("mixB")
                if "B" in phases:
                    with ExitStack() as esb:
                        set_pools(s=[0, 1, 2, 3, 4], o=[5, 6, 7])
                        qa = [sb(esb, "qa%d" % i, (96, T), BF16) for i in range(2)]
                        ka = [sb(esb, "ka%d" % i, (96, T), BF16) for i in range(2)]
                        va = [sb(esb, "va%d" % i, (128, NJ, 128), BF16) for i in range(2)]
                        b_qa, b_ka, b_va = S.bufs(2, "qa"), S.bufs(2, "ka"), S.bufs(2, "va")
                        pb = [sb(esb, "pb%d" % i, (128, 512), BF16) for i in range(4)]
                        b_pb = S.bufs(4, "pb")
                        rcs = [sb(esb, "rcs%d" % i, (64, 512), F32) for i in range(2)]
                        b_rcs = S.bufs(2, "rcs")
                        for i in range(2):
                            S.op("pool", lambda e, i=i: e.memset(va[i][:, :, 64:128], 1.0), writes=[b_va[i]])

                        def load_head(hh):
                            i = hh % 2
                            moba = hh < NH
                            h = hh % NH
                            S.dma("sp", "qa%d" % i, qa[i][0:64, :], (qm if moba else qf)[h], writes=[b_qa[i]])
                            S.dma("sp", "ka%d" % i, ka[i][0:64, :], (km if moba else kf)[h], writes=[b_ka[i]])
                            if moba:
                                S.dma("sp", "qa%d" % i, qa[i][64:80, :], nm[h], writes=[b_qa[i]])
                                S.dma("sp", "ka%d" % i, ka[i][64:80, :], C["blkoh"], writes=[b_ka[i]])
                            else:
                                S.dma("sp", "qa%d" % i, qa[i][64:67, :], cf[h * 3 * NJ:(h + 1) * 3 * NJ, :].rearrange("(r j) p -> r (j p)", r=3),
                                      writes=[b_qa[i]])
                                S.dma("sp", "ka%d" % i, ka[i][64:67, :], C["ones3"], writes=[b_ka[i]])
                            vcol = (0 if moba else 384) + h * 64
                            S.dma("sp", "va%d" % i, va[i][:, :, 0:64], vs[:, vcol:vcol + 64].rearrange("(j p) d -> p j d", p=128),
                                  writes=[b_va[i]])

                        steps = []
                        for hh in range(2 * NH):
                            moba = hh < NH
                            if moba:
                                for blk in range(NB):
                                    for n in range(blk + 1):
                                        steps.append(dict(hh=hh, q=blk, k=n, first=(n == 0), last=(n == blk)))
                            else:
                                for qt in range(NT):
                                    for j in range(4 * qt + 4):
                                        steps.append(dict(hh=hh, q=qt, k=j, first=(j == 0), last=(j == 4 * qt + 3)))

                        load_head(0)
                        state = {"O": None, "bO": None, "pbi": 0, "rci": 0}

                        def emit_qk(st):
                            hh = st["hh"]
                            i = hh % 2
                            moba = hh < NH
                            h = hh % NH
                            Sp, bSp = psum("s")
                            st["Sp"], st["bSp"] = Sp, bSp
                            if moba:
                                blk, n = st["q"], st["k"]
                                q0 = blk * 256

                                def f(e):
                                    ins = None
                                    for u in range(2):
                                        j = 2 * n + u
                                        o = Sp[:, u * 256:(u + 1) * 256]
                                        if n < blk:
                                            near = (j == 2 * blk - 1)
                                            ins = e.matmul(o, ka[i][0:80, j * 128:(j + 1) * 128], qa[i][0:80, q0:q0 + 256],
                                                           start=True, stop=not near)
                                            if near:
                                                ins = e.matmul(o, antij[:, :], gt[:, h, 256:512], start=False, stop=True)
                                        else:
                                            c0 = 128 if u == 0 else 0
                                            e.matmul(o, ka[i][0:64, j * 128:(j + 1) * 128], qa[i][0:64, q0:q0 + 256],
                                                     start=True, stop=False)
                                            ins = e.matmul(o, antij[:, :], gt[:, h, c0:c0 + 256], start=False, stop=True)
                                    return ins
                            else:
                                qt, j = st["q"], st["k"]
                                q0 = qt * 512

                                cl = max(0, j - 4 * qt) * 128
                                st["cl"] = cl

                                def f(e):
                                    diagt = j >= 4 * qt
                                    ins = e.matmul(Sp[:, cl:512], ka[i][0:67, j * 128:(j + 1) * 128], qa[i][0:67, q0 + cl:q0 + 512],
                                                   start=True, stop=not diagt)
                                    if diagt:
                                        c0 = 384 - (j - 4 * qt) * 128
                                        ins = e.matmul(Sp[:, cl:512], antij[:, :], gm[:, c0 + cl:c0 + 512], start=False, stop=True)
                                    return ins
                            S.op("pe", f, reads=[b_qa[i], b_ka[i], b_const, b_gt], writes=[bSp])

                        def emit_exp(st):
                            hh = st["hh"]
                            moba = hh < NH
                            h = hh % NH
                            k_ = state["pbi"] % 4
                            state["pbi"] += 1
                            st["pb"], st["bpb"] = pb[k_], b_pb[k_]
                            Sp = st["Sp"]
                            if moba:
                                S.op("act", lambda e: e.activation(pb[k_][:], Sp[:, :], AF.Exp), reads=[st["bSp"]], writes=[b_pb[k_]])
                            else:
                                j = st["k"]
                                cl = st["cl"]
                                S.op("act", lambda e: e.activation(pb[k_][:, cl:512], Sp[:, cl:512], AF.Exp, bias=negc[:, j, h:h + 1]),
                                     reads=[st["bSp"], b_negc], writes=[b_pb[k_]])

                        def emit_pv(st):
                            hh = st["hh"]
                            i = hh % 2
                            moba = hh < NH
                            h = hh % NH
                            if st["first"]:
                                state["O"], state["bO"] = psum("o")
                            O, bO = state["O"], state["bO"]
                            pbt = st["pb"]
                            if moba:
                                n = st["k"]

                                def f(e):
                                    e.matmul(O[:, 0:256], va[i][:, 2 * n, :], pbt[:, 0:256], start=st["first"], stop=False)
                                    return e.matmul(O[:, 0:256], va[i][:, 2 * n + 1, :], pbt[:, 256:512], start=False, stop=st["last"])
                                NQ = 256
                                q0 = st["q"] * 256
                            else:
                                j = st["k"]

                                cl = st["cl"]

                                def f(e):
                                    return e.matmul(O[:, cl:512], va[i][:, j, :], pbt[:, cl:512], start=st["first"], stop=st["last"])
                                NQ = 512
                                q0 = st["q"] * 512
                            S.op("pe", f, reads=[b_va[i], st["bpb"]], writes=[bO])
                            if st["last"]:
                                r_ = state["rci"] % 2
                                state["rci"] += 1
                                S.op("dve", lambda e: e.reciprocal(rcs[r_][0:64, 0:NQ], O[64:128, 0:NQ]), reads=[bO], writes=[b_rcs[r_]])
                                chunk = (0 if moba else 3) + h // 2
                                d0 = (h % 2) * 64
                                S.op("dve", lambda e: e.tensor_tensor(mixB[d0:d0 + 64, chunk, q0:q0 + NQ], O[0:64, 0:NQ], rcs[r_][0:64, 0:NQ], ALU.mult),
                                     reads=[bO, b_rcs[r_]], writes=[b_mixB])

                        nsteps = len(steps)
                        LA = 2
                        for si in range(min(LA, nsteps)):
                            emit_qk(steps[si])
                        for si in range(nsteps):
                            st = steps[si]
                            if st["first"] and st["q"] == 0 and st["hh"] + 1 < 2 * NH:
                                load_head(st["hh"] + 1)
                            if si + LA < nsteps:
                                emit_qk(steps[si + LA])
                            emit_exp(st)
                            emit_pv(st)
                    S.barrier()

                if "C" in phases:
                    with ExitStack() as esc:
                        set_pools(o=[0, 1, 2, 3])
                        xs = [sb(esc, "xc%d" % i, (128, 4, D), F32) for i in range(2)]
                        b_xs = S.bufs(2, "xc")
                        S.dma("sp", "xs0", xs[0][:], x_src[0:512, :].rearrange("(j p) d -> p j d", p=128), writes=[b_xs[0]])
                        for tt in range(NT):
                            cur = tt % 2
                            t0 = tt * 512
                            if tt + 1 < NT:
                                nx = (tt + 1) % 2
                                S.dma("sp", "xs%d" % nx, xs[nx][:], x_src[t0 + 512:t0 + 1024, :].rearrange("(j p) d -> p j d", p=128),
                                      writes=[b_xs[nx]])
                            for j in range(4):
                                for hf in range(2):
                                    po, bpo = psum("o")

                                    def f(e, j=j, hf=hf, po=po):
                                        ins = None
                                        for c in range(NCH):
                                            lhs = (mixA[:, c, t0 + j * 128:t0 + (j + 1) * 128] if c < 2
                                                   else mixB[:, c - 2, t0 + j * 128:t0 + (j + 1) * 128])
                                            ins = e.matmul(po[:, :], lhs, wo[:, c, hf * 512:(hf + 1) * 512],
                                                           start=(c == 0), stop=(c == NCH - 1))
                                        return ins
                                    S.op("pe", f, reads=[b_mixA, b_mixB] + b_wo, writes=[bpo])
                                    S.op("dve", lambda e, j=j, hf=hf, po=po, cur=cur: e.tensor_tensor(
                                        xs[cur][:, j, hf * 512:(hf + 1) * 512], po[:, :], xs[cur][:, j, hf * 512:(hf + 1) * 512], ALU.add),
                                        reads=[bpo, b_xs[cur]], writes=[b_xs[cur]])
                            S.dma("sp", "xs%d" % cur, xa[t0:t0 + 512, :].rearrange("(j p) d -> p j d", p=128), xs[cur][:], reads=[b_xs[cur]])
                    S.barrier()

            wo_cm.__exit__(None, None, None)
            if "D" in phases:
                with ExitStack() as esd:
                    set_pools(acc=[0, 1, 2, 3], gu=[4, 5], tp=[6, 7])
                    wgu = sb(esd, "wgu", (128, NCH, 2 * DFF), BF16)
                    pc2 = 16 + l * NCH
                    NG = NFC // 2
                    b_wf = S.bufs(NG, "wf")
                    wgu_v = P["w_gate_up"][l].rearrange("(c p) f -> p c f", p=128)
                    wd_v = P["w_down"][l].rearrange("(f p) d -> p f d", p=128)
                    for g_ in range(NG):
                        S.dma("pool", "wf%d" % g_, wgu[:, :, g_ * 256:(g_ + 1) * 256], wgu_v[:, :, g_ * 256:(g_ + 1) * 256],
                              writes=[b_wf[g_]], max_dma_last_dim=4096)
                        S.dma("pool", "wf%d" % g_, wgu[:, :, DFF + g_ * 256:DFF + (g_ + 1) * 256],
                              wgu_v[:, :, DFF + g_ * 256:DFF + (g_ + 1) * 256], writes=[b_wf[g_]], max_dma_last_dim=4096)

                    def fold_group(g_):
                        for o_ in (0, DFF):
                            S.op("dve", lambda e, o_=o_: e.tensor_tensor(
                                wgu[:, :, o_ + g_ * 256:o_ + (g_ + 1) * 256], wgu[:, :, o_ + g_ * 256:o_ + (g_ + 1) * 256],
                                prmT[:, pc2:pc2 + NCH].unsqueeze(2).to_broadcast([128, NCH, 256]), ALU.mult),
                                reads=[b_wf[g_], b_const], writes=[b_wf[g_]])
                    NT2 = T // 256
                    xs = [sb(esd, "xd%d" % i, (128, 2, D), F32) for i in range(3)]
                    b_xs = S.bufs(3, "xd")
                    hT = [sb(esd, "hD%d" % i, (128, NCH, 256), BF16) for i in range(2)]
                    b_hT = S.bufs(2, "hD")
                    tmp = make_norm_tmp(esd, 2)
                    sg = [sb(esd, "sg%d" % i, (128, 256), BF16) for i in range(2)]
                    b_sg = S.bufs(2, "sg")
                    aa = [sb(esd, "aa%d" % i, (128, 256), BF16) for i in range(3)]
                    b_aa = S.bufs(3, "aa")

                    def load_xd(tt):
                        k_ = tt % 3
                        S.dma("sp", "xs%d" % k_, xs[k_][:], xa[tt * 256:(tt + 1) * 256, :].rearrange("(j p) d -> p j d", p=128),
                              writes=[b_xs[k_]])

                    def n1(tt):
                        k_ = tt % 3
                        norm_part1([xs[k_][:, 0, :], xs[k_][:, 1, :]], [b_xs[k_], b_xs[k_]], 2, tmp)

                    def n2(tt):
                        norm_part2(2, hT[tt % 2], b_hT[tt % 2], tmp)

                    load_xd(0)
                    if NT2 > 1:
                        load_xd(1)
                    n1(0)
                    n2(0)
                    gi = 0
                    for tt in range(NT2):
                        cur = tt % 3
                        t0 = tt * 256
                        hTc, bhTc = hT[tt % 2], b_hT[tt % 2]
                        acc = [psum("acc") for _ in range(4)]

                        def emit_gu(fc, hTc=hTc, bhTc=bhTc):
                            pg, bpg = psum("gu")

                            def f(e):
                                for c in range(NCH):
                                    e.matmul(pg[:, 0:256], wgu[:, c, fc * 128:(fc + 1) * 128], hTc[:, c, :],
                                             start=(c == 0), stop=(c == NCH - 1))
                                ins = None
                                for c in range(NCH):
                                    ins = e.matmul(pg[:, 256:512], wgu[:, c, DFF + fc * 128:DFF + (fc + 1) * 128], hTc[:, c, :],
                                                   start=(c == 0), stop=(c == NCH - 1))
                                return ins
                            S.op("pe", f, reads=[bhTc, b_wf[fc // 2]], writes=[bpg])
                            return pg, bpg

                        if tt == 0:
                            fold_group(0)
                        nxt = emit_gu(0)
                        for fc in range(NFC):
                            pg, bpg = nxt
                            if fc + 1 < NFC:
                                if tt == 0 and (fc + 1) % 2 == 0:
                                    fold_group((fc + 1) // 2)
                                nxt = emit_gu(fc + 1)
                            if fc == 3 and tt + 1 < NT2:
                                n1(tt + 1)
                                if tt + 2 < NT2:
                                    load_xd(tt + 2)
                            if fc == 12 and tt + 1 < NT2:
                                n2(tt + 1)
                            sgi = sg[gi % 2]
                            bsgi = b_sg[gi % 2]
                            aai = aa[gi % 3]
                            baai = b_aa[gi % 3]
                            gi += 1
                            S.op("act", lambda e: e.activation(sgi[:], pg[:, 0:256], AF.Silu), reads=[bpg], writes=[bsgi])
                            S.op("dve", lambda e: e.tensor_tensor(aai[:], sgi[:], pg[:, 256:512], ALU.mult),
                                 reads=[bsgi, bpg], writes=[baai])

                            def fd(e):
                                ins = None
                                for j in range(2):
                                    for hf in range(2):
                                        ins = e.matmul(acc[j * 2 + hf][0][:, :], aai[:, j * 128:(j + 1) * 128],
                                                       wd[:, fc, hf * 512:(hf + 1) * 512], start=(fc == 0), stop=(fc == NFC - 1))
                                return ins
                            S.op("pe", fd, reads=[baai, b_wd], writes=[a_[1] for a_ in acc])
                        for j in range(2):
                            for hf in range(2):
                                a_, ba_ = acc[j * 2 + hf]
                                S.op("dve", lambda e: e.tensor_tensor(
                                    xs[cur][:, j, hf * 512:(hf + 1) * 512], a_[:, :], xs[cur][:, j, hf * 512:(hf + 1) * 512], ALU.add),
                                    reads=[ba_, b_xs[cur]], writes=[b_xs[cur]])
                        S.dma("sp", "xs%d" % cur, x_dst[t0:t0 + 256, :].rearrange("(j p) d -> p j d", p=128), xs[cur][:], reads=[b_xs[cur]])
                S.barrier()
            wd_cm.__exit__(None, None, None)

        S.barrier()
        S.emit()
    return nc


_CACHE = {}


def kernel(**inputs):
    T = 4096
    x = np.ascontiguousarray(np.asarray(inputs["x"], dtype=np.float32))
    B = x.shape[0]
    if "nc" not in _CACHE:
        _CACHE["nc"] = build_program(T=T, depth=2)
        _CACHE["consts"] = make_consts(T)
    nc = _CACHE["nc"]
    consts = _CACHE["consts"]
    base = {n: np.ascontiguousarray(np.asarray(inputs[n], dtype=np.float32)) for n, _ in PARAM_SPECS}
    for k, v in consts.items():
        base["c_" + k] = v
    in_maps = []
    for b in range(B):
        m = dict(base)
        m["x"] = x[b]
        in_maps.append(m)
    res = run_bass_kernel_spmd(nc, in_maps, core_ids=list(range(B)))
    return np.stack([np.asarray(r["out"], dtype=np.float32) for r in res.results], axis=0)
```
